# Optimizing a Trainium2 kernel written in Bass

```python
import math
import jax, jax.numpy as jnp
from jax import lax
import numpy as np

D_MODEL = 2048
BATCH = 2
SEQ = 8192
DEPTH = 1

GRID_W = 64
CTX_LEN = 256
HEAD_DIM = 128
A_HEADS = 8
A_KV_HEADS = 2
A_GROUPS = A_HEADS // A_KV_HEADS
WINDOW = 128
BLK = 128
B_HEADS = 8
B_QK_DIM = 64
B_V_DIM = 2 * B_QK_DIM
D_A = A_HEADS * HEAD_DIM
D_B = B_HEADS * B_V_DIM
FFN_HIDDEN = 5632
ROPE_BASE = 10000.0
LN_EPS = 1e-5
RMS_EPS = 1e-5
N_MOD = 9
DEEPNORM_ALPHA = (2.0 * DEPTH) ** 0.25
DEEPNORM_BETA = (8.0 * DEPTH) ** -0.25
A_Q_W = A_HEADS * HEAD_DIM
A_KV_W = A_KV_HEADS * HEAD_DIM
B_QK_W = B_HEADS * 2 * B_QK_DIM
B_V_W = B_HEADS * B_V_DIM
IN_OFFSETS = (A_Q_W, A_Q_W + A_KV_W, A_Q_W + 2 * A_KV_W, A_Q_W + 2 * A_KV_W + B_QK_W,
              A_Q_W + 2 * A_KV_W + 2 * B_QK_W, A_Q_W + 2 * A_KV_W + 2 * B_QK_W + B_V_W,
              A_Q_W + 2 * A_KV_W + 2 * B_QK_W + B_V_W + D_MODEL)
IN_WIDTH = A_Q_W + 2 * A_KV_W + 2 * B_QK_W + B_V_W + 2 * D_MODEL

kernel_name = 'hybrid_dit_window_gqa_diffattn_macaron'


def layer_norm(x, g, b):
    xf = x.astype(jnp.float32)
    mu = jnp.mean(xf, axis=-1, keepdims=True)
    var = jnp.mean(jnp.square(xf - mu), axis=-1, keepdims=True)
    return ((xf - mu) * lax.rsqrt(var + LN_EPS) * g + b).astype(x.dtype)


def rms_norm(x, g):
    xf = x.astype(jnp.float32)
    return (xf * lax.rsqrt(jnp.mean(jnp.square(xf), axis=-1, keepdims=True) + RMS_EPS) * g).astype(x.dtype)


def modulate(x, shift, scale):
    return x * (1 + scale) + shift


def swiglu(h, w_gate, w_up, w_down):
    return (jax.nn.silu(h @ w_gate) * (h @ w_up)) @ w_down


def axial_rope(n_tokens, dim):
    n_rows = n_tokens // GRID_W
    row = jnp.repeat(jnp.arange(n_rows, dtype=jnp.float32), GRID_W)
    col = jnp.tile(jnp.arange(GRID_W, dtype=jnp.float32), n_rows)
    quarter = dim // 4
    inv_freq = ROPE_BASE ** (-jnp.arange(quarter, dtype=jnp.float32) / quarter)
    ang = jnp.concatenate([row[:, None] * inv_freq, col[:, None] * inv_freq], axis=-1)
    return jnp.cos(ang), jnp.sin(ang)


def apply_rope(x, cos, sin):
    shape = (cos.shape[0],) + (1,) * (x.ndim - 3) + (cos.shape[-1],)
    cos = cos.reshape(shape).astype(x.dtype)
    sin = sin.reshape(shape).astype(x.dtype)
    x1, x2 = jnp.split(x, 2, axis=-1)
    return jnp.concatenate([x1 * cos - x2 * sin, x2 * cos + x1 * sin], axis=-1)


def split_in(p):
    bsz, n = p.shape[0], p.shape[1]
    qa, ka, va, qb, kb, vb, ga, gb = jnp.split(p, IN_OFFSETS, axis=-1)
    return (qa.reshape(bsz, n, A_KV_HEADS, A_GROUPS, HEAD_DIM),
            ka.reshape(bsz, n, A_KV_HEADS, HEAD_DIM),
            va.reshape(bsz, n, A_KV_HEADS, HEAD_DIM),
            qb.reshape(bsz, n, B_HEADS, 2, B_QK_DIM),
            kb.reshape(bsz, n, B_HEADS, 2, B_QK_DIM),
            vb.reshape(bsz, n, B_HEADS, B_V_DIM),
            ga, gb)


def softmax_with_sink(scores, sink):
    col = jnp.broadcast_to(sink.astype(jnp.float32).reshape(A_KV_HEADS, A_GROUPS, 1, 1),
                           scores.shape[:-1] + (1,))
    p = jax.nn.softmax(jnp.concatenate([scores, col], axis=-1), axis=-1)
    return p[..., :-1]


def window_gqa_latent(q, k, v, kc, vc, sink):
    b, s = q.shape[0], q.shape[1]
    n_ctx = kc.shape[1]
    nb = s // BLK
    scale = HEAD_DIM ** -0.5
    qb = q.reshape(b, nb, BLK, A_KV_HEADS, A_GROUPS, HEAD_DIM)

    def bands(t):
        tp = jnp.pad(t, ((0, 0), (BLK, BLK), (0, 0), (0, 0))).reshape(b, nb + 2, BLK, A_KV_HEADS, HEAD_DIM)
        return jnp.concatenate([tp[:, :-2], tp[:, 1:-1], tp[:, 2:]], axis=2)

    kw, vw = bands(k), bands(v)
    s_loc = jnp.einsum('bnqhgd,bnkhd->bnhgqk', qb, kw, preferred_element_type=jnp.float32) * scale
    s_ctx = jnp.einsum('bnqhgd,bchd->bnhgqc', qb, kc, preferred_element_type=jnp.float32) * scale
    blk_start = jnp.arange(nb)[:, None, None] * BLK
    qpos = blk_start + jnp.arange(BLK)[None, :, None]
    kpos = blk_start - BLK + jnp.arange(3 * BLK)[None, None, :]
    allowed = (jnp.abs(qpos - kpos) <= WINDOW) & (kpos >= 0) & (kpos < s)
    s_loc = jnp.where(allowed[None, :, None, None, :, :], s_loc, -jnp.inf)
    p = softmax_with_sink(jnp.concatenate([s_loc, s_ctx], axis=-1), sink).astype(v.dtype)
    p_loc, p_ctx = p[..., :3 * BLK], p[..., 3 * BLK:3 * BLK + n_ctx]
    o = (jnp.einsum('bnhgqk,bnkhd->bnqhgd', p_loc, vw)
         + jnp.einsum('bnhgqc,bchd->bnqhgd', p_ctx, vc))
    return o.reshape(b, s, D_A)


def gqa_context(qc, kc, vc, sink):
    b, n = qc.shape[0], qc.shape[1]
    sc = jnp.einsum('bqhgd,bkhd->bhgqk', qc, kc, preferred_element_type=jnp.float32) * HEAD_DIM ** -0.5
    p = softmax_with_sink(sc, sink).astype(vc.dtype)
    return jnp.einsum('bhgqk,bkhd->bqhgd', p, vc).reshape(b, n, D_A)


def diff_attend(q, k, v, lam):
    sc = jnp.einsum('bqhmd,bkhmd->bhmqk', q, k, preferred_element_type=jnp.float32) * B_QK_DIM ** -0.5
    p = jax.nn.softmax(sc, axis=-1)
    w = p[:, :, 0] - lam * p[:, :, 1]
    return jnp.einsum('bhqk,bkhd->bqhd', w.astype(v.dtype), v)


def diff_head_norm(o, g, lam_init):
    b, n = o.shape[0], o.shape[1]
    return (rms_norm(o, g) * (1.0 - lam_init)).reshape(b, n, D_B)


def diff_attention_latent(q, k, v, kc, vc, lam, subln_g, lam_init):
    b, s = q.shape[0], q.shape[1]
    nb = s // BLK
    k_all = jnp.concatenate([kc, k], axis=1)
    v_all = jnp.concatenate([vc, v], axis=1)
    q_blocks = jnp.moveaxis(q.reshape(b, nb, BLK, B_HEADS, 2, B_QK_DIM), 1, 0)
    o = lax.map(lambda qblk: diff_attend(qblk, k_all, v_all, lam), q_blocks)
    o = jnp.moveaxis(o, 0, 1).reshape(b, s, B_HEADS, B_V_DIM)
    return diff_head_norm(o, subln_g, lam_init)


def setup_inputs(seed: int = 0) -> dict:
    key = jax.random.key(seed)
    ks = jax.random.split(key, 32)
    f32 = jnp.float32
    L = DEPTH

    def nrm(k, shape, std):
        return jax.random.normal(k, shape, f32) * std

    return {
        'x': nrm(ks[0], (BATCH, SEQ, D_MODEL), 1.0),
        'c': nrm(ks[1], (BATCH, D_MODEL), 1.0),
        'ctx': nrm(ks[2], (BATCH, CTX_LEN, D_MODEL), 1.0),
        'c_ctx': nrm(ks[3], (D_MODEL,), 1.0),
        'w_mod': nrm(ks[4], (L, D_MODEL, N_MOD * D_MODEL), 0.5 * D_MODEL ** -0.5),
        'b_mod': nrm(ks[5], (L, N_MOD * D_MODEL), 0.02),
        'ffn1_w_gate': nrm(ks[6], (L, D_MODEL, FFN_HIDDEN), D_MODEL ** -0.5),
        'ffn1_w_up': nrm(ks[7], (L, D_MODEL, FFN_HIDDEN), D_MODEL ** -0.5),
        'ffn1_w_down': nrm(ks[8], (L, FFN_HIDDEN, D_MODEL), DEEPNORM_BETA * FFN_HIDDEN ** -0.5),
        'ln1_g': 1.0 + nrm(ks[9], (L, D_MODEL), 0.02),
        'ln1_b': nrm(ks[10], (L, D_MODEL), 0.02),
        'w_in': nrm(ks[11], (L, D_MODEL, IN_WIDTH), D_MODEL ** -0.5),
        'a_sink': nrm(ks[12], (L, A_HEADS), 0.5),
        'b_lambda_q1': nrm(ks[13], (L, B_QK_DIM), 0.1),
        'b_lambda_k1': nrm(ks[14], (L, B_QK_DIM), 0.1),
        'b_lambda_q2': nrm(ks[15], (L, B_QK_DIM), 0.1),
        'b_lambda_k2': nrm(ks[16], (L, B_QK_DIM), 0.1),
        'b_subln_g': 1.0 + nrm(ks[17], (L, B_V_DIM), 0.02),
        'w_branch_a': nrm(ks[18], (L, D_A, D_MODEL), D_A ** -0.5),
        'w_branch_b': nrm(ks[19], (L, D_B, D_MODEL), D_B ** -0.5),
        'w_out': nrm(ks[20], (L, D_MODEL, D_MODEL), DEEPNORM_BETA * D_MODEL ** -0.5),
        'ln2_g': 1.0 + nrm(ks[21], (L, D_MODEL), 0.02),
        'ln2_b': nrm(ks[22], (L, D_MODEL), 0.02),
        'ffn2_w_gate': nrm(ks[23], (L, D_MODEL, FFN_HIDDEN), D_MODEL ** -0.5),
        'ffn2_w_up': nrm(ks[24], (L, D_MODEL, FFN_HIDDEN), D_MODEL ** -0.5),
        'ffn2_w_down': nrm(ks[25], (L, FFN_HIDDEN, D_MODEL), DEEPNORM_BETA * FFN_HIDDEN ** -0.5),
        'ln3_g': 1.0 + nrm(ks[26], (L, D_MODEL), 0.02),
        'ln3_b': nrm(ks[27], (L, D_MODEL), 0.02),
    }


def reference(x, c, ctx, c_ctx, w_mod, b_mod, ffn1_w_gate, ffn1_w_up, ffn1_w_down, ln1_g, ln1_b,
              w_in, a_sink, b_lambda_q1, b_lambda_k1, b_lambda_q2, b_lambda_k2, b_subln_g,
              w_branch_a, w_branch_b, w_out, ln2_g, ln2_b, ffn2_w_gate, ffn2_w_up, ffn2_w_down,
              ln3_g, ln3_b):
    b, s, d = x.shape
    cos_a, sin_a = axial_rope(s, HEAD_DIM)
    cos_b, sin_b = axial_rope(s, B_QK_DIM)
    cx = ctx
    for l in range(DEPTH):
        last = l == DEPTH - 1
        mod = (jax.nn.silu(c) @ w_mod[l] + b_mod[l]).reshape(b, N_MOD, 1, d)
        modc = (jax.nn.silu(c_ctx) @ w_mod[l] + b_mod[l]).reshape(N_MOD, d)
        sh1, sc1, g1, sh2, sc2, g2, sh3, sc3, g3 = [mod[:, i] for i in range(N_MOD)]
        sh1c, sc1c, g1c, sh2c, sc2c, g2c, sh3c, sc3c, g3c = [modc[i] for i in range(N_MOD)]

        x = layer_norm(DEEPNORM_ALPHA * x + g1 * (0.5 * swiglu(modulate(x, sh1, sc1), ffn1_w_gate[l], ffn1_w_up[l], ffn1_w_down[l])),
                       ln1_g[l], ln1_b[l])
        cx = layer_norm(DEEPNORM_ALPHA * cx + g1c * (0.5 * swiglu(modulate(cx, sh1c, sc1c), ffn1_w_gate[l], ffn1_w_up[l], ffn1_w_down[l])),
                        ln1_g[l], ln1_b[l])

        h = modulate(x, sh2, sc2)
        hc = modulate(cx, sh2c, sc2c)
        qa, ka, va, qb, kb, vb, ga, gb = split_in(h @ w_in[l])
        qac, kac, vac, qbc, kbc, vbc, gac, gbc = split_in(hc @ w_in[l])
        lam_init = 0.8 - 0.6 * math.exp(-0.3 * l)
        lam = (jnp.exp(jnp.sum(b_lambda_q1[l].astype(jnp.float32) * b_lambda_k1[l].astype(jnp.float32)))
               - jnp.exp(jnp.sum(b_lambda_q2[l].astype(jnp.float32) * b_lambda_k2[l].astype(jnp.float32)))
               + lam_init)

        o_a = window_gqa_latent(apply_rope(qa, cos_a, sin_a), apply_rope(ka, cos_a, sin_a), va, kac, vac, a_sink[l])
        o_b = diff_attention_latent(apply_rope(qb, cos_b, sin_b), apply_rope(kb, cos_b, sin_b), vb, kbc, vbc,
                                    lam, b_subln_g[l], lam_init)
        merged = jax.nn.sigmoid(ga) * (o_a @ w_branch_a[l]) + jax.nn.sigmoid(gb) * (o_b @ w_branch_b[l])
        x_next = layer_norm(DEEPNORM_ALPHA * x + g2 * (merged @ w_out[l]), ln2_g[l], ln2_b[l])
        if not last:
            o_ac = gqa_context(qac, kac, vac, a_sink[l])
            o_bc = diff_head_norm(diff_attend(qbc, kbc, vbc, lam), b_subln_g[l], lam_init)
            merged_c = jax.nn.sigmoid(gac) * (o_ac @ w_branch_a[l]) + jax.nn.sigmoid(gbc) * (o_bc @ w_branch_b[l])
            cx = layer_norm(DEEPNORM_ALPHA * cx + g2c * (merged_c @ w_out[l]), ln2_g[l], ln2_b[l])
        x = x_next

        x = layer_norm(DEEPNORM_ALPHA * x + g3 * (0.5 * swiglu(modulate(x, sh3, sc3), ffn2_w_gate[l], ffn2_w_up[l], ffn2_w_down[l])),
                       ln3_g[l], ln3_b[l])
        if not last:
            cx = layer_norm(DEEPNORM_ALPHA * cx + g3c * (0.5 * swiglu(modulate(cx, sh3c, sc3c), ffn2_w_gate[l], ffn2_w_up[l], ffn2_w_down[l])),
                            ln3_g[l], ln3_b[l])
    return x
```

```python
import numpy as np
from contextlib import ExitStack
import concourse.bass as bass
import concourse.mybir as mybir
from concourse.bass_utils import run_bass_kernel_spmd

F32 = mybir.dt.float32
F32R = mybir.dt.float32r
BF = mybir.dt.bfloat16
AF = mybir.ActivationFunctionType
ALU = mybir.AluOpType
AX = mybir.AxisListType

D = 2048
DC = 16
HID = 5632
HC = 44
SEQ = 8192
NLOC = 2048
CTX = 256
TT_ = 512
NT = 4
ALPHA = 2.0 ** 0.25
LN_EPS = 1e-5
RMS_EPS = 1e-5
LAM_INIT = 0.2
KVROWS = 2560
R_KA, R_KB, R_VA, R_VB = 0, 256, 1280, 1536
NKB_B = 66
ENGS = ("pe", "act", "dve", "pool", "sp")

import os
import contextlib
DEBUG_OUT = bool(int(os.environ.get("KDEBUG", "0")))
KSTOP = int(os.environ.get("KSTOP", "4"))
NOCC = int(os.environ.get("NOCC", "0"))
KSUB = int(os.environ.get("KSUB", "9"))
KPROJ = int(os.environ.get("KPROJ", "3"))
KTILES = tuple(int(v) for v in os.environ.get("KTILES", "4,0,1,2,3").split(","))


class _Skip(Exception):
    pass


class Res:
    __slots__ = ("w", "r", "x")

    def __init__(self, excl=False):
        self.w = None
        self.r = {}
        self.x = excl


class Sched:
    def __init__(self):
        self.prog = {e: [] for e in ENGS}
        self.cnt = {e: 0 for e in ENGS}
        self.waited = {e: {} for e in ENGS}
        self.dcnt = {}

    def _deps(self, reads, writes):
        d = {}

        def add(k, v):
            if d.get(k, 0) < v:
                d[k] = v
        for r in reads:
            if r.w is not None:
                add(*r.w)
            if r.x:
                for k, v in r.r.items():
                    add(k, v)
        for w in writes:
            if w.w is not None:
                add(*w.w)
            for k, v in w.r.items():
                add(k, v)
        return d

    def _waits(self, eng, d):
        for sem, val in d.items():
            if sem == eng and eng == "pe":
                continue
            if self.waited[eng].get(sem, 0) >= val:
                continue
            self.waited[eng][sem] = val
            self.prog[eng].append(("w", sem, val))

    def group(self, eng, fns, reads=(), writes=()):
        self._waits(eng, self._deps(reads, writes))
        self.cnt[eng] += 1
        tok = (eng, self.cnt[eng])
        for f in fns[:-1]:
            self.prog[eng].append(("o", f, None, 0))
        self.prog[eng].append(("o", fns[-1], eng, 1))
        for r in reads:
            if r.r.get(eng, 0) < tok[1]:
                r.r[eng] = tok[1]
        for w in writes:
            w.w = tok
            w.r = {}
        return tok

    def op(self, eng, fn, reads=(), writes=()):
        return self.group(eng, [fn], reads, writes)

    def dma(self, q, fn, sem, reads=(), writes=(), inc=16):
        self._waits(q, self._deps(reads, writes))
        self.dcnt[sem] = self.dcnt.get(sem, 0) + inc
        tok = (sem, self.dcnt[sem])
        self.prog[q].append(("o", fn, sem, inc))
        for r in reads:
            if r.r.get(sem, 0) < tok[1]:
                r.r[sem] = tok[1]
        for w in writes:
            w.w = tok
            w.r = {}
        return tok

    def barrier(self):
        allv = dict(self.cnt)
        allv.update(self.dcnt)
        for e in ENGS:
            d = {k: v for k, v in allv.items() if v > 0 and k != e}
            self._waits(e, d)


def MM(out, lhsT, rhs, start, stop):
    return lambda e: e.matmul(out, lhsT, rhs, start=start, stop=stop)


def ACT(out, in_, func, **kw):
    return lambda e: e.activation(out=out, in_=in_, func=func, **kw)


def TS(out, in0, s1, s2, op0, op1=None):
    if op1 is None:
        return lambda e: e.tensor_scalar(out=out, in0=in0, scalar1=s1, scalar2=None, op0=op0)
    return lambda e: e.tensor_scalar(out=out, in0=in0, scalar1=s1, scalar2=s2, op0=op0, op1=op1)


def TTo(out, in0, in1, op):
    return lambda e: e.tensor_tensor(out=out, in0=in0, in1=in1, op=op)


def STT(out, in0, scalar, in1, op0, op1):
    return lambda e: e.scalar_tensor_tensor(out=out, in0=in0, scalar=scalar, in1=in1, op0=op0, op1=op1)


def DMA(out, in_):
    return lambda e: e.dma_start(out=out, in_=in_)


def CP(out, in_):
    return lambda e: e.tensor_copy(out=out, in_=in_)


def RCP(out, in_):
    return lambda e: e.reciprocal(out=out, in_=in_)


def MSET(ap, v):
    return lambda e: e.memset(ap, v)


class Ring:
    def __init__(self, tile, n, name):
        self.tile = tile
        self.n = n
        self.i = 0
        self.name = name
        self.res = [Res() for _ in range(n)]

    def next(self):
        k = self.i % self.n
        self.i += 1
        return k


def build_program():
    nc = bass.Bass("TRN2", target_bir_lowering=False)
    S = Sched()

    def din(name, shape, dt=F32):
        return nc.dram_tensor(name, list(shape), dt, kind="ExternalInput").ap()

    xT_d = din("xT", [D, NLOC])
    ctxT_d = din("ctxT", [D, CTX])
    cc_d = din("cc", [128, 32])
    bmod_d = din("bmodc", [128, 144])
    lnp_d = din("lnp", [128, 96])
    subg_d = din("subg", [128, 1])
    sink_d = din("sinkb", [128, 8])
    lamv_d = din("lamv", [128, 256])
    wmod_d = din("w_mod", [D, 9 * D])
    wg1_d = din("wg1", [HC, 128, D])
    wu1_d = din("wu1", [HC, 128, D])
    wd1_d = din("wd1", [DC, 128, HID])
    wg2_d = din("wg2", [HC, 128, D])
    wu2_d = din("wu2", [HC, 128, D])
    wd2_d = din("wd2", [DC, 128, HID])
    win_d = din("w_in", [68, 128, D])
    wba_d = din("w_ba", [DC, 128, 1024])
    wbb_d = din("w_bb", [DC, 128, 1024])
    wout_d = din("w_out", [DC, 128, D])
    ropeA_d = din("ropeA", [2, 128, NLOC])
    ropeB_d = din("ropeB", [2, 128, NLOC])
    pmat_d = din("pmat", [2, 128, 128])
    mask3_d = din("mask3", [128, 384])
    medge_d = din("medge", [128, 8 * 128])
    outT_d = nc.dram_tensor("outT", [D, NLOC], F32, kind="ExternalOutput").ap()

    x1_d = nc.dram_tensor("x1s", [D, NLOC], F32).ap()
    q_d = nc.dram_tensor("qs", [16, 128, NLOC], BF).ap()
    kc_d = nc.dram_tensor("kcs", [10, 128, CTX], BF).ap()
    vc_d = nc.dram_tensor("vcs", [10, CTX, 128], BF).ap()
    CH_ROWS = (1024, 1024, 512)
    kvl_h = [[nc.dram_tensor(f"kvl{t}_{c}", [CH_ROWS[c], TT_], BF) for c in range(3)] for t in range(NT)]
    kva_h = [[nc.dram_tensor(f"kva{t}_{c}", [4 * CH_ROWS[c], TT_], BF) for c in range(3)] for t in range(NT)]

    def L(t, row0):
        c, l0 = row0 // 1024, row0 % 1024
        return kvl_h[t][c].ap()[l0:l0 + 128, :]

    def G(t, r, row0):
        c, l0 = row0 // 1024, row0 % 1024
        b0 = r * CH_ROWS[c] + l0
        return kva_h[t][c].ap()[b0:b0 + 128, :]
    dbg = {}
    if DEBUG_OUT:
        dbg["x1"] = nc.dram_tensor("dbg_x1", [D, NLOC], F32, kind="ExternalOutput").ap()
        dbg["ot"] = nc.dram_tensor("dbg_ot", [16, 128, NLOC], BF, kind="ExternalOutput").ap()
        dbg["q"] = nc.dram_tensor("dbg_q", [16, 128, NLOC], BF, kind="ExternalOutput").ap()
        dbg["kv"] = nc.dram_tensor("dbg_kv", [4 * KVROWS, TT_], BF, kind="ExternalOutput").ap()

    R_x1d = [Res() for _ in range(NT)]
    R_qd = [[Res() for _ in range(NT)] for _ in range(16)]
    R_kcd = [Res() for _ in range(10)]
    R_vcd = [Res() for _ in range(10)]
    R_kvl = [Res() for _ in range(NT)]
    R_kva = [Res() for _ in range(NT)]

    with ExitStack() as es:
        def sb(name, shape, dt):
            return es.enter_context(nc.sbuf_tensor("t_" + name, list(shape), dt))

        banks = [es.enter_context(nc.psum_tensor(f"bank{i}", [128, 512], F32)) for i in range(8)]
        R_bank = [Res(True) for _ in range(8)]

        modT = sb("modT", [128, 144, 2], F32)
        lnp = sb("lnp", [128, 96], F32)
        subg = sb("subg", [128, 1], F32)
        esink = sb("esink", [128, 8], F32)
        lamt = sb("lamt", [128, 4], F32)
        ones_r = sb("ones_r", [128, 128], F32R)
        ones_b = sb("ones_b", [128, 128], BF)
        epsln = sb("epsln", [128, 2], F32)
        pmat = sb("pmat", [128, 2, 128], F32R)
        mask3 = sb("mask3", [128, 384], BF)
        medge = sb("medge", [128, 8, 128], BF)
        R_const = Res()

        def mod(i, c, n):
            return modT[:, i * 16 + c, n:n + 1]

        def lncol(v, c):
            return lnp[:, v * 16 + c:v * 16 + c + 1]

        with ExitStack() as es0:
            def sb0(name, shape, dt):
                return es0.enter_context(nc.sbuf_tensor("t0_" + name, list(shape), dt))
            cc_t = sb0("cc_t", [128, 32], F32)
            s_t = sb0("s_t", [128, 16, 2], F32R)
            bmod_t = sb0("bmod_t", [128, 144], F32)
            sink_t = sb0("sink_t", [128, 8], F32)
            lamv_t = sb0("lamv_t", [128, 256], F32)
            lamp = sb0("lamp", [128, 128], F32)
            lams = sb0("lams", [128, 4], F32)
            m3f = sb0("m3f", [128, 384], F32)
            mef = sb0("mef", [128, 1024], F32)
            wm = sb0("wm", [128, 2, 16, 1024], F32R)
            R_wm = [Res(), Res()]
            R_cc, R_s, R_misc, R_lam = Res(), Res(), Res(), Res()

            S.dma("sp", DMA(cc_t[:], cc_d[:, :]), "ld0", writes=[R_cc])
            S.dma("sp", DMA(bmod_t[:], bmod_d[:, :]), "ld1", writes=[R_misc])
            S.dma("sp", DMA(lnp[:], lnp_d[:, :]), "ld1", writes=[R_misc])
            S.dma("sp", DMA(subg[:], subg_d[:, :]), "ld1", writes=[R_misc])
            S.dma("sp", DMA(sink_t[:], sink_d[:, :]), "ld1", writes=[R_misc])
            S.dma("sp", DMA(lamv_t[:], lamv_d[:, :]), "ld1", writes=[R_misc])
            S.dma("sp", DMA(m3f[:], mask3_d[:, :]), "ld1", writes=[R_misc])
            S.dma("sp", DMA(mef[:], medge_d[:, :]), "ld1", writes=[R_misc])
            S.dma("pool", DMA(pmat[:, 0, :], pmat_d[0]), "ld2", writes=[R_const])
            S.dma("pool", DMA(pmat[:, 1, :], pmat_d[1]), "ld2", writes=[R_const])
            onesf = sb0("onesf", [128, 128], F32)
            R_onesf = Res()
            S.op("dve", MSET(onesf[:], 1.0), writes=[R_onesf])
            S.op("dve", CP(ones_r[:], onesf[:]), reads=[R_onesf], writes=[R_const])
            S.op("dve", CP(ones_b[:], onesf[:]), reads=[R_onesf], writes=[R_const])
            S.op("dve", MSET(epsln[:, 0:1], LN_EPS), writes=[R_const])
            S.op("dve", MSET(epsln[:, 1:2], RMS_EPS), writes=[R_const])
            S.op("dve", CP(mask3[:], m3f[:]), reads=[R_misc], writes=[R_const])
            S.op("dve", CP(medge[:].rearrange("p a b -> p (a b)"), mef[:]), reads=[R_misc], writes=[R_const])
            S.op("act", ACT(s_t[:].rearrange("p a b -> p (a b)"), cc_t[:], AF.Silu), reads=[R_cc], writes=[R_s])
            S.op("act", ACT(esink[:], sink_t[:], AF.Exp), reads=[R_misc], writes=[R_const])
            S.op("dve", TTo(lamp[:, 0:64], lamv_t[:, 0:64], lamv_t[:, 64:128], ALU.mult), reads=[R_misc], writes=[R_lam])
            S.op("dve", TTo(lamp[:, 64:128], lamv_t[:, 128:192], lamv_t[:, 192:256], ALU.mult), reads=[R_misc, R_lam], writes=[R_lam])
            S.op("dve", lambda e: e.reduce_sum(out=lams[:, 0:1], in_=lamp[:, 0:64], axis=AX.X), reads=[R_lam], writes=[R_lam])
            S.op("dve", lambda e: e.reduce_sum(out=lams[:, 1:2], in_=lamp[:, 64:128], axis=AX.X), reads=[R_lam], writes=[R_lam])
            S.op("act", ACT(lams[:, 2:4], lams[:, 0:2], AF.Exp), reads=[R_lam], writes=[R_lam])
            S.op("dve", TTo(lamt[:, 0:1], lams[:, 2:3], lams[:, 3:4], ALU.subtract), reads=[R_lam], writes=[R_lam])
            S.op("dve", TS(lamt[:, 0:1], lamt[:, 0:1], LAM_INIT, None, ALU.add), reads=[R_lam], writes=[R_lam])
            S.op("dve", TS(lamt[:, 1:2], lamt[:, 0:1], -1.0, None, ALU.mult), reads=[R_lam], writes=[R_const])

            psm = banks[0]
            for blk in range(18):
                buf = blk % 2
                for kc in range(16):
                    S.dma("pool", DMA(wm[:, buf, kc, :], wmod_d[kc * 128:(kc + 1) * 128, blk * 1024:(blk + 1) * 1024]),
                          f"wm{buf}", writes=[R_wm[buf]])
                for jj in range(8):
                    col = (blk * 8 + jj) * 2
                    S.group("pe", [MM(psm[:, col:col + 2], wm[:, buf, kc, jj * 128:(jj + 1) * 128], s_t[:, kc, :], kc == 0, kc == 15)
                                   for kc in range(16)], reads=[R_wm[buf], R_s], writes=[R_bank[0]])
            S.op("dve", CP(modT[:].rearrange("p a b -> p (a b)"), psm[:, 0:288]), reads=[R_bank[0]], writes=[R_const])
            for n in range(2):
                S.op("dve", TTo(modT[:, :, n], modT[:, :, n], bmod_t[:], ALU.add), reads=[R_misc, R_const], writes=[R_const])
            for i in (1, 4, 7):
                S.op("dve", TS(modT[:, i * 16:(i + 1) * 16, :], modT[:, i * 16:(i + 1) * 16, :], 1.0, None, ALU.add),
                     reads=[R_const], writes=[R_const])
            for i in (2, 8):
                S.op("dve", TS(modT[:, i * 16:(i + 1) * 16, :], modT[:, i * 16:(i + 1) * 16, :], 0.5, None, ALU.mult),
                     reads=[R_const], writes=[R_const])
        S.barrier()


        class FfnCtx:
            pass

        def make_ffn(esx, pf):
            F = FfnCtx()

            def sbx(name, shape, dt):
                return esx.enter_context(nc.sbuf_tensor(pf + name, list(shape), dt))
            F.xt = sbx("xt", [128, DC, TT_], F32)
            F.ht = sbx("ht", [128, DC, TT_], BF)
            F.actt = sbx("actt", [128, HC, TT_], BF)
            NW = 6
            F.wring_t = sbx("wring", [128, NW, D], BF)
            F.wring = Ring(F.wring_t, NW, pf + "wr")
            NWD = 2
            F.wdring_t = sbx("wdring", [128, NWD, HID], BF)
            F.wdring = Ring(F.wdring_t, NWD, pf + "wd")
            F.sgt = sbx("sgt", [128, 2, TT_], F32)
            F.zr = sbx("zr", [128, 2, TT_], F32R)
            F.zsq = sbx("zsq", [128, 2, TT_], F32R)
            F.st_mean = sbx("st_mean", [128, TT_], F32)
            F.st_a = sbx("st_a", [128, TT_], F32)
            F.st_rstd = sbx("st_rstd", [128, TT_], F32)
            F.st_nmr = sbx("st_nmr", [128, TT_], F32)
            F.lt1 = sbx("lt1", [128, 2, TT_], F32)
            F.lt2 = sbx("lt2", [128, 2, TT_], F32)
            F.R_xt = [Res() for _ in range(DC)]
            F.R_ht = [Res() for _ in range(DC)]
            F.R_act = [Res() for _ in range(HC)]
            F.R_sg = [Res(), Res()]
            F.R_zr = [Res(), Res()]
            F.R_zsq = [Res(), Res()]
            F.R_st = Res()
            F.R_lt1 = [Res(), Res()]
            F.R_lt2 = [Res(), Res()]
            cnt = {"g": 0, "y": 0, "z": 0, "l": 0}
            xt, ht, actt = F.xt, F.ht, F.actt

            def wload(src_ap, width=D):
                k = F.wring.next()
                S.dma("pool", DMA(F.wring_t[:, k, 0:width], src_ap), f"{pf}wr{k}", writes=[F.wring.res[k]])
                return k
            F.wload = wload

            def modulate_to_h(T, i_sc, i_sh, n):
                for c in range(DC):
                    S.op("dve", TS(ht[:, c, :T], xt[:, c, :T], mod(i_sc, c, n), mod(i_sh, c, n), ALU.mult, ALU.add),
                         reads=[F.R_xt[c], R_const], writes=[F.R_ht[c]])
            F.modulate_to_h = modulate_to_h

            def ln_finish(T, v, post=None):
                psS, psQ = banks[6], banks[7]
                S.op("act", ACT(F.st_mean[:, :T], psS[:, :T], AF.Identity, scale=1.0 / D), reads=[R_bank[6]], writes=[F.R_st])
                S.op("dve", TTo(F.st_a[:, :T], F.st_mean[:, :T], F.st_mean[:, :T], ALU.mult), reads=[F.R_st], writes=[F.R_st])
                S.op("dve", STT(F.st_a[:, :T], psQ[:, :T], 1.0 / D, F.st_a[:, :T], ALU.mult, ALU.subtract),
                     reads=[R_bank[7], F.R_st], writes=[F.R_st])
                S.op("act", ACT(F.st_a[:, :T], F.st_a[:, :T], AF.Sqrt, bias=epsln[:, 0:1], scale=1.0),
                     reads=[F.R_st, R_const], writes=[F.R_st])
                S.op("dve", RCP(F.st_rstd[:, :T], F.st_a[:, :T]), reads=[F.R_st], writes=[F.R_st])
                S.op("dve", STT(F.st_nmr[:, :T], F.st_mean[:, :T], -1.0, F.st_rstd[:, :T], ALU.mult, ALU.mult),
                     reads=[F.R_st], writes=[F.R_st])
                for c in range(DC):
                    b = cnt["l"] % 2
                    cnt["l"] += 1
                    S.op("dve", TTo(F.lt1[:, b, :T], xt[:, c, :T], F.st_rstd[:, :T], ALU.mult),
                         reads=[F.R_xt[c], F.R_st], writes=[F.R_lt1[b]])
                    S.op("dve", TTo(F.lt2[:, b, :T], F.lt1[:, b, :T], F.st_nmr[:, :T], ALU.add),
                         reads=[F.R_lt1[b], F.R_st], writes=[F.R_lt2[b]])
                    S.op("act", ACT(xt[:, c, :T], F.lt2[:, b, :T], AF.Identity, bias=lncol(v + 1, c), scale=lncol(v, c)),
                         reads=[F.R_lt2[b], R_const], writes=[F.R_xt[c]])
                    if post is not None:
                        post(c)
            F.ln_finish = ln_finish

            def residual_stats(T, dc, by, gscal):
                S.op("dve", STT(xt[:, dc, :T], banks[by][:, :T], gscal, xt[:, dc, :T], ALU.mult, ALU.add),
                     reads=[R_bank[by], F.R_xt[dc], R_const], writes=[F.R_xt[dc]])
                b = cnt["z"] % 2
                cnt["z"] += 1
                S.op("act", ACT(F.zr[:, b, :T], xt[:, dc, :T], AF.Copy), reads=[F.R_xt[dc]], writes=[F.R_zr[b]])
                S.op("act", ACT(F.zsq[:, b, :T], xt[:, dc, :T], AF.Square), reads=[F.R_xt[dc]], writes=[F.R_zsq[b]])
                S.op("pe", MM(banks[6][:, :T], ones_r[:], F.zr[:, b, :T], dc == 0, dc == DC - 1),
                     reads=[F.R_zr[b], R_const], writes=[R_bank[6]])
                S.op("pe", MM(banks[7][:, :T], ones_r[:], F.zsq[:, b, :T], dc == 0, dc == DC - 1),
                     reads=[F.R_zsq[b], R_const], writes=[R_bank[7]])
            F.residual_stats = residual_stats

            def scale_x(T):
                for c in range(DC):
                    S.op("act", ACT(xt[:, c, :T], xt[:, c, :T], AF.Identity, scale=ALPHA), reads=[F.R_xt[c]], writes=[F.R_xt[c]])
            F.scale_x = scale_x

            def ffn(T, n, i0, wg_d, wu_d, wd_d, v, post=None):
                modulate_to_h(T, i0 + 1, i0, n)
                scale_x(T)
                if KSUB <= 1:
                    return
                for hc in range(HC):
                    kg = wload(wg_d[hc])
                    ku = wload(wu_d[hc])
                    b = cnt["g"] % 2
                    cnt["g"] += 1
                    bg, bu = b, 2 + b
                    S.group("pe", [MM(banks[bg][:, :T], F.wring_t[:, kg, kc * 128:(kc + 1) * 128], ht[:, kc, :T], kc == 0, kc == DC - 1)
                                   for kc in range(DC)], reads=[F.wring.res[kg]] + F.R_ht, writes=[R_bank[bg]])
                    S.group("pe", [MM(banks[bu][:, :T], F.wring_t[:, ku, kc * 128:(kc + 1) * 128], ht[:, kc, :T], kc == 0, kc == DC - 1)
                                   for kc in range(DC)], reads=[F.wring.res[ku]] + F.R_ht, writes=[R_bank[bu]])
                    S.op("act", ACT(F.sgt[:, b, :T], banks[bg][:, :T], AF.Silu), reads=[R_bank[bg]], writes=[F.R_sg[b]])
                    S.op("dve", TTo(actt[:, hc, :T], F.sgt[:, b, :T], banks[bu][:, :T], ALU.mult),
                         reads=[F.R_sg[b], R_bank[bu]], writes=[F.R_act[hc]])
                if KSUB <= 2:
                    return
                for dc in range(DC):
                    k = F.wdring.next()
                    S.dma("pool", DMA(F.wdring_t[:, k, :], wd_d[dc]), f"{pf}wd{k}", writes=[F.wdring.res[k]])
                    by = 4 + cnt["y"] % 2
                    cnt["y"] += 1
                    S.group("pe", [MM(banks[by][:, :T], F.wdring_t[:, k, hc * 128:(hc + 1) * 128], actt[:, hc, :T], hc == 0, hc == HC - 1)
                                   for hc in range(HC)], reads=[F.wdring.res[k]] + F.R_act, writes=[R_bank[by]])
                    residual_stats(T, dc, by, mod(i0 + 2, dc, n))
                if KSUB <= 3:
                    return
                ln_finish(T, v, post)
            F.ffn = ffn
            return F

        with contextlib.suppress(_Skip), ExitStack() as es2:
            if KSTOP < 1:
                raise _Skip()
            F = make_ffn(es2, "a_")
            xt, ht = F.xt, F.ht

            def sb2(name, shape, dt):
                return es2.enter_context(nc.sbuf_tensor("t2_" + name, list(shape), dt))
            ropet = sb2("ropet", [128, 4, TT_], F32)
            qraw = sb2("qraw", [128, 2, TT_], F32R)
            rt1 = sb2("rt1", [128, 2, TT_], F32)
            rt2 = sb2("rt2", [128, 2, TT_], F32)
            outb = sb2("outb", [128, 4, TT_], BF)
            voutb = sb2("voutb", [128, 2, 4, 128], BF)
            R_rope = Res()
            R_qraw = [Res(), Res()]
            R_rt1 = [Res(), Res()]
            R_rt2 = [Res(), Res()]
            R_outb = [Res() for _ in range(4)]
            R_vout = [Res(), Res()]
            ca = {"p": 0, "r": 0, "o": 0, "v": 0}

            def proj_fm(T, oc):
                k = F.wload(win_d[oc])
                bp = ca["p"] % 4
                ca["p"] += 1
                S.group("pe", [MM(banks[bp][:, :T], F.wring_t[:, k, kc * 128:(kc + 1) * 128], ht[:, kc, :T], kc == 0, kc == DC - 1)
                               for kc in range(DC)], reads=[F.wring.res[k]] + F.R_ht, writes=[R_bank[bp]])
                return bp

            def store_fm(T, bo, dst_ap, dst_res):
                S.dma("sp", DMA(dst_ap, outb[:, bo, :T]), f"so{bo}", reads=[R_outb[bo]], writes=dst_res)

            def proj_rope(T, oc, which, dst_ap, dst_res):
                bp = proj_fm(T, oc)
                b = ca["r"] % 2
                ca["r"] += 1
                br = 4 + b
                S.op("act", ACT(qraw[:, b, :T], banks[bp][:, :T], AF.Copy), reads=[R_bank[bp]], writes=[R_qraw[b]])
                S.op("pe", MM(banks[br][:, :T], pmat[:, which, :], qraw[:, b, :T], True, True),
                     reads=[R_qraw[b], R_const], writes=[R_bank[br]])
                S.op("dve", TTo(rt1[:, b, :T], banks[bp][:, :T], ropet[:, 2 * which, :T], ALU.mult),
                     reads=[R_bank[bp], R_rope], writes=[R_rt1[b]])
                S.op("dve", TTo(rt2[:, b, :T], banks[br][:, :T], ropet[:, 2 * which + 1, :T], ALU.mult),
                     reads=[R_bank[br], R_rope], writes=[R_rt2[b]])
                bo = ca["o"] % 4
                ca["o"] += 1
                S.op("dve", TTo(outb[:, bo, :T], rt1[:, b, :T], rt2[:, b, :T], ALU.add),
                     reads=[R_rt1[b], R_rt2[b]], writes=[R_outb[bo]])
                store_fm(T, bo, dst_ap, dst_res)

            def proj_plain(T, oc, dst_ap, dst_res):
                bp = proj_fm(T, oc)
                bo = ca["o"] % 4
                ca["o"] += 1
                S.op("act", ACT(outb[:, bo, :T], banks[bp][:, :T], AF.Copy), reads=[R_bank[bp]], writes=[R_outb[bo]])
                store_fm(T, bo, dst_ap, dst_res)

            def proj_v(T, oc, dst_ap, dst_res):
                k = F.wload(win_d[oc])
                bv = 6 + ca["v"] % 2
                ntb = T // 128
                for tb in range(ntb):
                    S.group("pe", [MM(banks[bv][:, tb * 128:(tb + 1) * 128], ht[:, kc, tb * 128:(tb + 1) * 128],
                                      F.wring_t[:, k, kc * 128:(kc + 1) * 128], kc == 0, kc == DC - 1)
                                   for kc in range(DC)], reads=[F.wring.res[k]] + F.R_ht, writes=[R_bank[bv]])
                b = ca["v"] % 2
                ca["v"] += 1
                S.op("act", ACT(voutb[:, b, 0:ntb, :], banks[bv][:, 0:T].rearrange("p (a b) -> p a b", b=128), AF.Copy),
                     reads=[R_bank[bv]], writes=[R_vout[b]])
                S.dma("sp", DMA(dst_ap.rearrange("(a p) d -> p a d", p=128), voutb[:, b, 0:ntb, :]), f"sv{b}",
                      reads=[R_vout[b]], writes=dst_res)

            def vview(ap2d):
                return ap2d.rearrange("r (x d) -> (r x) d", d=128)

            for tile in KTILES:
                is_ctx = tile == 4
                T = CTX if is_ctx else TT_
                n = 1 if is_ctx else 0
                src = ctxT_d if is_ctx else xT_d
                t0 = 0 if is_ctx else tile * TT_
                S.dma("sp", DMA(xt[:, :, :T], src.rearrange("(c p) t -> p c t", p=128)[:, :, t0:t0 + T]), "ldx", writes=F.R_xt)
                if not is_ctx:
                    S.dma("sp", DMA(ropet[:, 0:2, :], ropeA_d.rearrange("a p t -> p a t")[:, :, t0:t0 + T]), "ldr", writes=[R_rope])
                    S.dma("sp", DMA(ropet[:, 2:4, :], ropeB_d.rearrange("a p t -> p a t")[:, :, t0:t0 + T]), "ldr", writes=[R_rope])
                F.ffn(T, n, 0, wg1_d, wu1_d, wd1_d, 0)
                if KSUB <= 4:
                    continue
                if not is_ctx:
                    S.dma("sp", DMA(x1_d.rearrange("(c p) t -> p c t", p=128)[:, :, t0:t0 + T], xt[:, :, :T]), "stx",
                          reads=F.R_xt, writes=[R_x1d[tile]])
                    if DEBUG_OUT:
                        S.dma("sp", DMA(dbg["x1"].rearrange("(c p) t -> p c t", p=128)[:, :, t0:t0 + T], xt[:, :, :T]), "stdbg",
                              reads=F.R_xt)
                F.modulate_to_h(T, 4, 3, n)
                if KSUB <= 5:
                    continue
                if is_ctx:
                    for j, oc in enumerate([8, 9] + list(range(20, 28))):
                        proj_plain(T, oc, kc_d[j], [R_kcd[j]])
                    for j, oc in enumerate([10, 11] + list(range(28, 36))):
                        proj_v(T, oc, vc_d[j], [R_vcd[j]])
                else:
                    for hq in range(8 if KPROJ & 1 else 0):
                        proj_rope(T, hq, 0, q_d[hq][:, t0:t0 + T], [R_qd[hq][tile]])
                    for g in range(2 if KPROJ & 1 else 0):
                        proj_rope(T, 8 + g, 0, L(tile, R_KA + g * 128), [R_kvl[tile]])
                    for h in range(8 if KPROJ & 1 else 0):
                        proj_rope(T, 12 + h, 1, q_d[8 + h][:, t0:t0 + T], [R_qd[8 + h][tile]])
                    for h in range(8 if KPROJ & 1 else 0):
                        proj_rope(T, 20 + h, 1, L(tile, R_KB + h * 128), [R_kvl[tile]])
                    for g in range(2 if KPROJ & 2 else 0):
                        proj_v(T, 10 + g, vview(L(tile, R_VA + g * 128)), [R_kvl[tile]])
                    for h in range(8 if KPROJ & 2 else 0):
                        proj_v(T, 28 + h, vview(L(tile, R_VB + h * 128)), [R_kvl[tile]])
                    d = {k_: v_ for k_, v_ in S.dcnt.items() if k_.startswith("so") or k_.startswith("sv")}
                    S._waits("pool", d)
                    if not NOCC and KSTOP >= 2:
                        for c3 in range(3):
                            S.dma("pool", (lambda tl, cc_: (lambda e: e.collective_compute(
                                "AllGather", ALU.bypass, replica_groups=[[0, 1, 2, 3], [4, 5, 6, 7]],
                                ins=[kvl_h[tl][cc_].ap().opt()], outs=[kva_h[tl][cc_].ap().opt()])))(tile, c3),
                                f"cc{tile}_{c3}", reads=[R_kvl[tile]], writes=[R_kva[tile]], inc=1)
        S.barrier()

        ot_d = nc.dram_tensor("ots", [16, 128, NLOC], BF).ap()
        R_otd = [[Res() for _ in range(NT)] for _ in range(16)]
        with contextlib.suppress(_Skip), ExitStack() as es3:
            if KSTOP < 3:
                raise _Skip()
            def sb3(name, shape, dt):
                return es3.enter_context(nc.sbuf_tensor("t3_" + name, list(shape), dt))
            NK = 256 + SEQ
            KT = sb3("KT", [128, 2, NK], BF)
            VT = sb3("VT", [128, 2, NKB_B, 128], BF)
            QT = sb3("QT", [128, 2, TT_], BF)
            pT = sb3("pT", [128, 4, TT_], BF)
            af1 = sb3("af1", [128, 2, TT_], F32)
            af2 = sb3("af2", [128, 2, TT_], F32)
            af3 = sb3("af3", [128, TT_], F32)
            osq = sb3("osq", [128, TT_], F32R)
            ob = sb3("ob", [128, 2, TT_], BF)
            R_KT = [Res(), Res()]
            R_VT = [Res(), Res()]
            R_QT = [Res(), Res()]
            R_pT = [Res() for _ in range(4)]
            R_af1 = [Res(), Res()]
            R_af2 = [Res(), Res()]
            R_af3 = Res()
            R_osq = Res()
            R_ob = [Res(), Res()]
            aa = {"kv": 0, "q": 0, "s": 0, "p": 0, "o": 0}
            SCALE_A = 128.0 ** -0.5
            SCALE_B = 64.0 ** -0.5

            def load_q(qc, gq):
                b = aa["q"] % 2
                aa["q"] += 1
                S.dma("sp", DMA(QT[:, b, :], q_d[qc][:, gq * TT_:(gq + 1) * TT_]), f"lq{b}",
                      reads=[R_qd[qc][gq]], writes=[R_QT[b]])
                return b

            def store_o(qc, gq, bo):
                S.dma("sp", DMA(ot_d[qc][:, gq * TT_:(gq + 1) * TT_], ob[:, bo, :]), f"sob{bo}",
                      reads=[R_ob[bo]], writes=[R_otd[qc][gq]])
                if DEBUG_OUT:
                    S.dma("sp", DMA(dbg["ot"][qc][:, gq * TT_:(gq + 1) * TT_], ob[:, bo, :]), f"sob{bo}", reads=[R_ob[bo]])

            for g in range(2):
                kb_ = aa["kv"] % 2
                aa["kv"] += 1
                S.dma("sp", DMA(KT[:, kb_, 0:CTX], kc_d[g]), f"lk{kb_}", reads=[R_kcd[g]], writes=[R_KT[kb_]])
                S.dma("sp", DMA(VT[:, kb_, 0:2, :], vc_d[g].rearrange("(a p) d -> p a d", p=128)), f"lv{kb_}",
                      reads=[R_vcd[g]], writes=[R_VT[kb_]])
                for t in range(NT):
                    S.dma("sp", DMA(KT[:, kb_, CTX + t * TT_:CTX + (t + 1) * TT_], L(t, R_KA + g * 128)),
                          f"lk{kb_}", reads=[R_kvl[t]], writes=[R_KT[kb_]])
                    S.dma("sp", DMA(VT[:, kb_, 2 + 4 * t:2 + 4 * (t + 1), :],
                                    vview(L(t, R_VA + g * 128)).rearrange("(a p) d -> p a d", p=128)),
                          f"lv{kb_}", reads=[R_kvl[t]], writes=[R_VT[kb_]])
                for r in range(4):
                    base = r * KVROWS
                    S.dma("sp", DMA(KT[:, kb_, CTX + NLOC + r * 128:CTX + NLOC + (r + 1) * 128],
                                    G(3, r, R_KA + g * 128)[:, 384:512]),
                          f"lk{kb_}", reads=[R_kva[3]], writes=[R_KT[kb_]])
                    S.dma("sp", DMA(KT[:, kb_, CTX + NLOC + (4 + r) * 128:CTX + NLOC + (5 + r) * 128],
                                    G(0, r, R_KA + g * 128)[:, 0:128]),
                          f"lk{kb_}", reads=[R_kva[0]], writes=[R_KT[kb_]])
                    S.dma("sp", DMA(VT[:, kb_, 18 + r, :],
                                    vview(G(3, r, R_VA + g * 128))[384:512, :]),
                          f"lv{kb_}", reads=[R_kva[3]], writes=[R_VT[kb_]])
                    S.dma("sp", DMA(VT[:, kb_, 22 + r, :],
                                    vview(G(0, r, R_VA + g * 128))[0:128, :]),
                          f"lv{kb_}", reads=[R_kva[0]], writes=[R_VT[kb_]])
                for gi in range(4):
                    qc = g * 4 + gi
                    for gq in range(4):
                        qb_ = load_q(qc, gq)
                        n0 = gq * 4
                        items = [(0, 0, TT_, None), (1, 0, TT_, None)]
                        for kbl in range(max(n0 - 1, 0), min(n0 + 4, 15) + 1):
                            qlo = max(kbl - 1, n0)
                            qhi = min(kbl + 1, n0 + 3)
                            mlo = (qlo - (kbl - 1)) * 128
                            mhi = (qhi - (kbl - 1) + 1) * 128
                            items.append((2 + kbl, (qlo - n0) * 128, (qhi - n0 + 1) * 128, mask3[:, mlo:mhi]))
                        if n0 == 0:
                            for r in range(4):
                                items.append((18 + r, 0, 128, medge[:, r, :]))
                        if n0 + 3 == 15:
                            for r in range(4):
                                items.append((22 + r, 384, 512, medge[:, 4 + r, :]))
                        nit = len(items)
                        for ii, (kblk, c0, c1, msk) in enumerate(items):
                            bs = aa["s"] % 4
                            aa["s"] += 1
                            S.op("pe", MM(banks[bs][:, c0:c1], KT[:, kb_, kblk * 128:(kblk + 1) * 128], QT[:, qb_, c0:c1], True, True),
                                 reads=[R_KT[kb_], R_QT[qb_]], writes=[R_bank[bs]])
                            bp = aa["p"] % 4
                            aa["p"] += 1
                            S.op("act", ACT(pT[:, bp, c0:c1], banks[bs][:, c0:c1], AF.Exp, scale=SCALE_A),
                                 reads=[R_bank[bs]], writes=[R_pT[bp]])
                            if msk is not None:
                                S.op("dve", TTo(pT[:, bp, c0:c1], pT[:, bp, c0:c1], msk, ALU.mult),
                                     reads=[R_pT[bp], R_const], writes=[R_pT[bp]])
                            S.op("pe", MM(banks[4][:, c0:c1], VT[:, kb_, kblk, :], pT[:, bp, c0:c1], ii == 0, ii == nit - 1),
                                 reads=[R_VT[kb_], R_pT[bp]], writes=[R_bank[4]])
                            S.op("pe", MM(banks[6][:, c0:c1], ones_b[:], pT[:, bp, c0:c1], ii == 0, ii == nit - 1),
                                 reads=[R_pT[bp], R_const], writes=[R_bank[6]])
                        b = aa["o"] % 2
                        aa["o"] += 1
                        S.op("dve", TS(af1[:, b, :], banks[6][:, :], esink[:, qc:qc + 1], None, ALU.add),
                             reads=[R_bank[6], R_const], writes=[R_af1[b]])
                        S.op("dve", RCP(af1[:, b, :], af1[:, b, :]), reads=[R_af1[b]], writes=[R_af1[b]])
                        S.op("dve", TTo(ob[:, b, :], banks[4][:, :], af1[:, b, :], ALU.mult),
                             reads=[R_bank[4], R_af1[b]], writes=[R_ob[b]])
                        store_o(qc, gq, b)

            for h in range(8):
                kb_ = aa["kv"] % 2
                aa["kv"] += 1
                S.dma("sp", DMA(KT[:, kb_, 0:CTX], kc_d[2 + h]), f"lk{kb_}", reads=[R_kcd[2 + h]], writes=[R_KT[kb_]])
                S.dma("sp", DMA(VT[:, kb_, 0:2, :], vc_d[2 + h].rearrange("(a p) d -> p a d", p=128)), f"lv{kb_}",
                      reads=[R_vcd[2 + h]], writes=[R_VT[kb_]])
                for r in range(4):
                    base = r * KVROWS
                    for t in range(NT):
                        k0 = CTX + r * NLOC + t * TT_
                        S.dma("sp", DMA(KT[:, kb_, k0:k0 + TT_], G(t, r, R_KB + h * 128)),
                              f"lk{kb_}", reads=[R_kva[t]], writes=[R_KT[kb_]])
                        S.dma("sp", DMA(VT[:, kb_, k0 // 128:k0 // 128 + 4, :],
                                        vview(G(t, r, R_VB + h * 128)).rearrange("(a p) d -> p a d", p=128)),
                              f"lv{kb_}", reads=[R_kva[t]], writes=[R_VT[kb_]])
                qc = 8 + h
                for gq in range(4):
                    qb_ = load_q(qc, gq)
                    for kblk in range(NKB_B):
                        bs0 = (aa["s"] % 2) * 2
                        aa["s"] += 1
                        bs1 = bs0 + 1
                        S.op("pe", MM(banks[bs0][:, :], KT[0:64, kb_, kblk * 128:(kblk + 1) * 128], QT[0:64, qb_, :], True, True),
                             reads=[R_KT[kb_], R_QT[qb_]], writes=[R_bank[bs0]])
                        S.op("pe", MM(banks[bs1][:, :], KT[64:128, kb_, kblk * 128:(kblk + 1) * 128], QT[64:128, qb_, :], True, True),
                             reads=[R_KT[kb_], R_QT[qb_]], writes=[R_bank[bs1]])
                        bp0 = (aa["p"] % 2) * 2
                        aa["p"] += 1
                        bp1 = bp0 + 1
                        S.op("act", ACT(pT[:, bp0, :], banks[bs0][:, :], AF.Exp, scale=SCALE_B), reads=[R_bank[bs0]], writes=[R_pT[bp0]])
                        S.op("act", ACT(pT[:, bp1, :], banks[bs1][:, :], AF.Exp, scale=SCALE_B), reads=[R_bank[bs1]], writes=[R_pT[bp1]])
                        first, last = kblk == 0, kblk == NKB_B - 1
                        S.op("pe", MM(banks[4][:, :], VT[:, kb_, kblk, :], pT[:, bp0, :], first, last),
                             reads=[R_VT[kb_], R_pT[bp0]], writes=[R_bank[4]])
                        S.op("pe", MM(banks[5][:, :], VT[:, kb_, kblk, :], pT[:, bp1, :], first, last),
                             reads=[R_VT[kb_], R_pT[bp1]], writes=[R_bank[5]])
                        S.op("pe", MM(banks[6][:, :], ones_b[:], pT[:, bp0, :], first, last),
                             reads=[R_pT[bp0], R_const], writes=[R_bank[6]])
                        S.op("pe", MM(banks[7][:, :], ones_b[:], pT[:, bp1, :], first, last),
                             reads=[R_pT[bp1], R_const], writes=[R_bank[7]])
                    S.op("dve", RCP(af1[:, 0, :], banks[6][:, :]), reads=[R_bank[6]], writes=[R_af1[0]])
                    S.op("dve", RCP(af1[:, 1, :], banks[7][:, :]), reads=[R_bank[7]], writes=[R_af1[1]])
                    S.op("dve", TTo(af2[:, 0, :], banks[4][:, :], af1[:, 0, :], ALU.mult), reads=[R_bank[4], R_af1[0]], writes=[R_af2[0]])
                    S.op("dve", TTo(af2[:, 1, :], banks[5][:, :], af1[:, 1, :], ALU.mult), reads=[R_bank[5], R_af1[1]], writes=[R_af2[1]])
                    S.op("dve", STT(af3[:, :], af2[:, 1, :], lamt[:, 1:2], af2[:, 0, :], ALU.mult, ALU.add),
                         reads=[R_af2[0], R_af2[1], R_const], writes=[R_af3])
                    S.op("act", ACT(osq[:, :], af3[:, :], AF.Square), reads=[R_af3], writes=[R_osq])
                    S.op("pe", MM(banks[0][:, :], ones_r[:], osq[:, :], True, True), reads=[R_osq, R_const], writes=[R_bank[0]])
                    S.op("act", ACT(af1[:, 0, :], banks[0][:, :], AF.Sqrt, bias=epsln[:, 1:2], scale=1.0 / 128.0),
                         reads=[R_bank[0], R_const], writes=[R_af1[0]])
                    S.op("dve", RCP(af1[:, 0, :], af1[:, 0, :]), reads=[R_af1[0]], writes=[R_af1[0]])
                    S.op("dve", TTo(af2[:, 0, :], af3[:, :], af1[:, 0, :], ALU.mult), reads=[R_af3, R_af1[0]], writes=[R_af2[0]])
                    b = aa["o"] % 2
                    aa["o"] += 1
                    S.op("dve", TS(ob[:, b, :], af2[:, 0, :], subg[:, 0:1], 1.0 - LAM_INIT, ALU.mult, ALU.mult),
                         reads=[R_af2[0], R_const], writes=[R_ob[b]])
                    store_o(qc, gq, b)
        S.barrier()

        out_toks = []
        with contextlib.suppress(_Skip), ExitStack() as es4:
            if KSTOP < 4:
                raise _Skip()
            F = make_ffn(es4, "c_")
            xt, ht, actt = F.xt, F.ht, F.actt

            def sb4(name, shape, dt):
                return es4.enter_context(nc.sbuf_tensor("t4_" + name, list(shape), dt))
            sga = sb4("sga", [128, TT_], F32)
            sgb = sb4("sgb", [128, TT_], F32)
            mt1 = sb4("mt1", [128, TT_], F32)
            mt2 = sb4("mt2", [128, TT_], F32)
            R_sga, R_sgb, R_mt1, R_mt2 = Res(), Res(), Res(), Res()
            T = TT_
            for tile in range(NT):
                t0 = tile * TT_
                S.dma("sp", DMA(xt[:, :, :], x1_d.rearrange("(c p) t -> p c t", p=128)[:, :, t0:t0 + T]), "ldx2",
                      reads=[R_x1d[tile]], writes=F.R_xt)
                S.dma("sp", DMA(actt[:, 16:32, :], ot_d.rearrange("c p t -> p c t")[:, :, t0:t0 + T]), "ldo",
                      reads=[R_otd[c][tile] for c in range(16)], writes=F.R_act[16:32])
                F.modulate_to_h(T, 4, 3, 0)
                F.scale_x(T)
                for j in range(DC):
                    kga = F.wload(win_d[36 + j])
                    kgb = F.wload(win_d[52 + j])
                    kba = F.wload(wba_d[j], 1024)
                    kbb = F.wload(wbb_d[j], 1024)
                    S.group("pe", [MM(banks[0][:, :], F.wring_t[:, kga, kc * 128:(kc + 1) * 128], ht[:, kc, :], kc == 0, kc == DC - 1)
                                   for kc in range(DC)], reads=[F.wring.res[kga]] + F.R_ht, writes=[R_bank[0]])
                    S.group("pe", [MM(banks[1][:, :], F.wring_t[:, kgb, kc * 128:(kc + 1) * 128], ht[:, kc, :], kc == 0, kc == DC - 1)
                                   for kc in range(DC)], reads=[F.wring.res[kgb]] + F.R_ht, writes=[R_bank[1]])
                    S.group("pe", [MM(banks[2][:, :], F.wring_t[:, kba, kc * 128:(kc + 1) * 128], actt[:, 16 + kc, :], kc == 0, kc == 7)
                                   for kc in range(8)], reads=[F.wring.res[kba]] + F.R_act[16:24], writes=[R_bank[2]])
                    S.group("pe", [MM(banks[3][:, :], F.wring_t[:, kbb, kc * 128:(kc + 1) * 128], actt[:, 24 + kc, :], kc == 0, kc == 7)
                                   for kc in range(8)], reads=[F.wring.res[kbb]] + F.R_act[24:32], writes=[R_bank[3]])
                    S.op("act", ACT(sga[:, :], banks[0][:, :], AF.Sigmoid), reads=[R_bank[0]], writes=[R_sga])
                    S.op("act", ACT(sgb[:, :], banks[1][:, :], AF.Sigmoid), reads=[R_bank[1]], writes=[R_sgb])
                    S.op("dve", TTo(mt1[:, :], sga[:, :], banks[2][:, :], ALU.mult), reads=[R_sga, R_bank[2]], writes=[R_mt1])
                    S.op("dve", TTo(mt2[:, :], sgb[:, :], banks[3][:, :], ALU.mult), reads=[R_sgb, R_bank[3]], writes=[R_mt2])
                    S.op("dve", TTo(actt[:, j, :], mt1[:, :], mt2[:, :], ALU.add), reads=[R_mt1, R_mt2], writes=[F.R_act[j]])
                for dc in range(DC):
                    k = F.wload(wout_d[dc])
                    by = 4 + dc % 2
                    S.group("pe", [MM(banks[by][:, :], F.wring_t[:, k, kc * 128:(kc + 1) * 128], actt[:, kc, :], kc == 0, kc == DC - 1)
                                   for kc in range(DC)], reads=[F.wring.res[k]] + F.R_act[0:16], writes=[R_bank[by]])
                    F.residual_stats(T, dc, by, mod(5, dc, 0))
                F.ln_finish(T, 2)
                F.ffn(T, 0, 6, wg2_d, wu2_d, wd2_d, 4)
                out_toks.append(S.dma("sp", DMA(outT_d.rearrange("(c p) t -> p c t", p=128)[:, :, t0:t0 + T], xt[:, :, :]), "sto",
                                      reads=F.R_xt))
        if DEBUG_OUT and KSTOP >= 1:
            S.barrier()
            S.dma("sp", DMA(dbg["q"][:, :, 0:TT_], q_d[:, :, 0:TT_]), "stdbg")
        if KSTOP < 4:
            S.barrier()
            srcd = (x1_d if KSUB >= 5 else xT_d) if KSTOP >= 1 else wmod_d[0:D, 0:NLOC]
            if len(KTILES) < 5 and KSTOP >= 1:
                S.dma("sp", DMA(outT_d[:, 0:TT_], srcd[:, 0:TT_]), "sto")
            else:
                S.dma("sp", DMA(outT_d[:, :], srcd[:, :] if KSTOP >= 1 else srcd), "sto")
        S.barrier()

        sem_names = list(ENGS) + sorted(S.dcnt.keys())
        sems = {nm: es.enter_context(nc.semaphore("s_" + nm)) for nm in sem_names}
        block = es.enter_context(nc.Block())

        def replay(eng_name):
            def body(e):
                for it in S.prog[eng_name]:
                    if it[0] == "w":
                        e.wait_ge(sems[it[1]], it[2])
                    else:
                        ins = it[1](e)
                        if it[2] is not None:
                            if it[2].startswith("cc"):
                                ins.then_inc(sems[it[2]])
                            else:
                                ins.then_inc(sems[it[2]], it[3])
            return body
        block.tensor(replay("pe"))
        block.scalar(replay("act"))
        block.vector(replay("dve"))
        block.gpsimd(replay("pool"))
        block.sync(replay("sp"))
    return nc


def _tile_w(W):
    K, N = W.shape
    return np.ascontiguousarray(W.reshape(K // 128, 128, N // 128, 128).transpose(2, 1, 0, 3).reshape(N // 128, 128, K))


def _cols(v):
    return np.ascontiguousarray(np.asarray(v, np.float32).reshape(-1, 128).T)


def _rope_tables(pos, dim):
    quarter = dim // 4
    inv = (10000.0 ** (-np.arange(quarter, dtype=np.float32) / quarter)).astype(np.float32)
    row = (pos // 64).astype(np.float32)
    col = (pos % 64).astype(np.float32)
    ang = np.concatenate([row[:, None] * inv, col[:, None] * inv], axis=-1).astype(np.float32)
    cos, sin = np.cos(ang).astype(np.float32), np.sin(ang).astype(np.float32)
    half = dim // 2
    idx = np.arange(128) % half
    return np.ascontiguousarray(np.stack([cos[:, idx].T, sin[:, idx].T]).astype(np.float32))


def _pmat(dim):
    half = dim // 2
    M = np.zeros((128, 128), np.float32)
    for p in range(128):
        i = p % dim
        if i < half:
            M[p, p + half] = -1.0
        else:
            M[p, p - half] = 1.0
    return np.ascontiguousarray(M.T)


_CACHE = {}


def kernel(x, c, ctx, c_ctx, w_mod, b_mod, ffn1_w_gate, ffn1_w_up, ffn1_w_down, ln1_g, ln1_b,
           w_in, a_sink, b_lambda_q1, b_lambda_k1, b_lambda_q2, b_lambda_k2, b_subln_g,
           w_branch_a, w_branch_b, w_out, ln2_g, ln2_b, ffn2_w_gate, ffn2_w_up, ffn2_w_down,
           ln3_g, ln3_b):
    f = lambda a: np.asarray(a, np.float32)
    x, c, ctx, c_ctx = f(x), f(c), f(ctx), f(c_ctx)
    shared = {
        "bmodc": _cols(f(b_mod)[0]),
        "lnp": np.ascontiguousarray(np.concatenate([_cols(f(v)[0]) for v in (ln1_g, ln1_b, ln2_g, ln2_b, ln3_g, ln3_b)], axis=1)),
        "subg": np.ascontiguousarray(f(b_subln_g)[0].reshape(128, 1)),
        "sinkb": np.ascontiguousarray(np.broadcast_to(f(a_sink)[0][None, :], (128, 8))),
        "lamv": np.ascontiguousarray(np.broadcast_to(np.concatenate(
            [f(b_lambda_q1)[0], f(b_lambda_k1)[0], f(b_lambda_q2)[0], f(b_lambda_k2)[0]])[None, :], (128, 256))),
        "w_mod": np.ascontiguousarray(f(w_mod)[0]),
        "wg1": _tile_w(f(ffn1_w_gate)[0]), "wu1": _tile_w(f(ffn1_w_up)[0]), "wd1": _tile_w(f(ffn1_w_down)[0]),
        "wg2": _tile_w(f(ffn2_w_gate)[0]), "wu2": _tile_w(f(ffn2_w_up)[0]), "wd2": _tile_w(f(ffn2_w_down)[0]),
        "w_in": _tile_w(f(w_in)[0]), "w_ba": _tile_w(f(w_branch_a)[0]), "w_bb": _tile_w(f(w_branch_b)[0]),
        "w_out": _tile_w(f(w_out)[0]),
        "pmat": np.ascontiguousarray(np.stack([_pmat(128), _pmat(64)])),
    }
    jj, ii = np.meshgrid(np.arange(128), np.arange(128), indexing="ij")
    tri_le = (jj <= ii).astype(np.float32)
    tri_ge = (jj >= ii).astype(np.float32)
    shared["mask3"] = np.ascontiguousarray(np.concatenate([tri_le, np.ones((128, 128), np.float32), tri_ge], axis=1))
    in_maps = []
    for r in range(8):
        b, qd = r // 4, r % 4
        m = dict(shared)
        m["xT"] = np.ascontiguousarray(x[b, qd * NLOC:(qd + 1) * NLOC, :].T)
        m["ctxT"] = np.ascontiguousarray(ctx[b].T)
        m["cc"] = np.ascontiguousarray(np.stack([_cols(c[b]), _cols(c_ctx)], axis=2).reshape(128, 32))
        pos = np.arange(qd * NLOC, (qd + 1) * NLOC)
        m["ropeA"] = _rope_tables(pos, 128)
        m["ropeB"] = _rope_tables(pos, 64)
        me = np.zeros((128, 8, 128), np.float32)
        if qd > 0:
            me[:, qd - 1, :] = tri_ge
        if qd < 3:
            me[:, 4 + qd + 1, :] = tri_le
        m["medge"] = np.ascontiguousarray(me.reshape(128, 1024))
        in_maps.append(m)
    if os.environ.get("KSIM"):
        _CACHE["in_maps"] = in_maps
        return None
    if "nc" not in _CACHE:
        _CACHE["nc"] = build_program()
    ncores = int(os.environ.get("KCORES", "8"))
    res = run_bass_kernel_spmd(_CACHE["nc"], in_maps[:ncores], core_ids=list(range(ncores)))
    _CACHE["last"] = res
    out = np.empty((2, SEQ, D), np.float32)
    for r in range(ncores):
        b, qd = r // 4, r % 4
        out[b, qd * NLOC:(qd + 1) * NLOC, :] = np.asarray(res.results[r]["outT"], np.float32).T
    return out
```

```python
import numpy as np
from contextlib import ExitStack
import concourse.bass as bass
import concourse.mybir as mybir
from concourse.bass_utils import run_bass_kernel_spmd

F32 = mybir.dt.float32
F32R = mybir.dt.float32r
BF = mybir.dt.bfloat16
AF = mybir.ActivationFunctionType
ALU = mybir.AluOpType
AX = mybir.AxisListType

D = 2048
DC = 16
HID = 5632
HC = 44
SEQ = 8192
NLOC = 2048
CTX = 256
TT_ = 512
NT = 4
ALPHA = 2.0 ** 0.25
LN_EPS = 1e-5
RMS_EPS = 1e-5
LAM_INIT = 0.2
KVROWS = 2560
R_KA, R_KB, R_VA, R_VB = 0, 256, 1280, 1536
NKB_B = 66
ENGS = ("pe", "act", "dve", "pool", "sp")

import os
import contextlib
DEBUG_OUT = bool(int(os.environ.get("KDEBUG", "0")))
KSTOP = int(os.environ.get("KSTOP", "4"))
NOCC = int(os.environ.get("NOCC", "0"))
KSUB = int(os.environ.get("KSUB", "9"))
KPROJ = int(os.environ.get("KPROJ", "3"))
KTILES = tuple(int(v) for v in os.environ.get("KTILES", "4,0,1,2,3").split(","))


class _Skip(Exception):
    pass


class Res:
    __slots__ = ("w", "r", "x")

    def __init__(self, excl=False):
        self.w = None
        self.r = {}
        self.x = excl


class Sched:
    def __init__(self):
        self.prog = {e: [] for e in ENGS}
        self.cnt = {e: 0 for e in ENGS}
        self.waited = {e: {} for e in ENGS}
        self.dcnt = {}

    def _deps(self, reads, writes):
        d = {}

        def add(k, v):
            if d.get(k, 0) < v:
                d[k] = v
        for r in reads:
            if r.w is not None:
                add(*r.w)
            if r.x:
                for k, v in r.r.items():
                    add(k, v)
        for w in writes:
            if w.w is not None:
                add(*w.w)
            for k, v in w.r.items():
                add(k, v)
        return d

    def _waits(self, eng, d):
        for sem, val in d.items():
            if sem == eng and eng == "pe":
                continue
            if self.waited[eng].get(sem, 0) >= val:
                continue
            self.waited[eng][sem] = val
            self.prog[eng].append(("w", sem, val))

    def group(self, eng, fns, reads=(), writes=()):
        self._waits(eng, self._deps(reads, writes))
        self.cnt[eng] += 1
        tok = (eng, self.cnt[eng])
        for f in fns[:-1]:
            self.prog[eng].append(("o", f, None, 0))
        self.prog[eng].append(("o", fns[-1], eng, 1))
        for r in reads:
            if r.r.get(eng, 0) < tok[1]:
                r.r[eng] = tok[1]
        for w in writes:
            w.w = tok
            w.r = {}
        return tok

    def op(self, eng, fn, reads=(), writes=()):
        return self.group(eng, [fn], reads, writes)

    def dma(self, q, fn, sem, reads=(), writes=(), inc=16):
        self._waits(q, self._deps(reads, writes))
        self.dcnt[sem] = self.dcnt.get(sem, 0) + inc
        tok = (sem, self.dcnt[sem])
        self.prog[q].append(("o", fn, sem, inc))
        for r in reads:
            if r.r.get(sem, 0) < tok[1]:
                r.r[sem] = tok[1]
        for w in writes:
            w.w = tok
            w.r = {}
        return tok

    def barrier(self):
        allv = dict(self.cnt)
        allv.update(self.dcnt)
        for e in ENGS:
            d = {k: v for k, v in allv.items() if v > 0 and k != e}
            self._waits(e, d)


def MM(out, lhsT, rhs, start, stop):
    return lambda e: e.matmul(out, lhsT, rhs, start=start, stop=stop)


def ACT(out, in_, func, **kw):
    return lambda e: e.activation(out=out, in_=in_, func=func, **kw)


def TS(out, in0, s1, s2, op0, op1=None):
    if op1 is None:
        return lambda e: e.tensor_scalar(out=out, in0=in0, scalar1=s1, scalar2=None, op0=op0)
    return lambda e: e.tensor_scalar(out=out, in0=in0, scalar1=s1, scalar2=s2, op0=op0, op1=op1)


def TTo(out, in0, in1, op):
    return lambda e: e.tensor_tensor(out=out, in0=in0, in1=in1, op=op)


def STT(out, in0, scalar, in1, op0, op1):
    return lambda e: e.scalar_tensor_tensor(out=out, in0=in0, scalar=scalar, in1=in1, op0=op0, op1=op1)


def DMA(out, in_):
    return lambda e: e.dma_start(out=out, in_=in_)


def CP(out, in_):
    return lambda e: e.tensor_copy(out=out, in_=in_)


def RCP(out, in_):
    return lambda e: e.reciprocal(out=out, in_=in_)


def MSET(ap, v):
    return lambda e: e.memset(ap, v)


class Ring:
    def __init__(self, tile, n, name):
        self.tile = tile
        self.n = n
        self.i = 0
        self.name = name
        self.res = [Res() for _ in range(n)]

    def next(self):
        k = self.i % self.n
        self.i += 1
        return k


def build_program():
    nc = bass.Bass("TRN2", target_bir_lowering=False)
    S = Sched()

    def din(name, shape, dt=F32):
        return nc.dram_tensor(name, list(shape), dt, kind="ExternalInput").ap()

    xT_d = din("xT", [D, NLOC])
    ctxT_d = din("ctxT", [D, CTX])
    cc_d = din("cc", [128, 32])
    bmod_d = din("bmodc", [128, 144])
    lnp_d = din("lnp", [128, 96])
    subg_d = din("subg", [128, 1])
    sink_d = din("sinkb", [128, 8])
    lamv_d = din("lamv", [128, 256])
    wmod_d = din("w_mod", [D, 9 * D])
    wg1_d = din("wg1", [HC, 128, D])
    wu1_d = din("wu1", [HC, 128, D])
    wd1_d = din("wd1", [DC, 128, HID])
    wg2_d = din("wg2", [HC, 128, D])
    wu2_d = din("wu2", [HC, 128, D])
    wd2_d = din("wd2", [DC, 128, HID])
    win_d = din("w_in", [68, 128, D])
    wba_d = din("w_ba", [DC, 128, 1024])
    wbb_d = din("w_bb", [DC, 128, 1024])
    wout_d = din("w_out", [DC, 128, D])
    ropeA_d = din("ropeA", [2, 128, NLOC])
    ropeB_d = din("ropeB", [2, 128, NLOC])
    pmat_d = din("pmat", [2, 128, 128])
    mask3_d = din("mask3", [128, 384])
    medge_d = din("medge", [128, 8 * 128])
    ident_d = din("ident", [2, 2])
    outT_d = nc.dram_tensor("outT", [D, NLOC], F32, kind="ExternalOutput").ap()

    x1_d = nc.dram_tensor("x1s", [D, NLOC], F32).ap()
    q_d = nc.dram_tensor("qs", [16, 128, NLOC], BF).ap()
    kc_d = nc.dram_tensor("kcs", [10, 128, CTX], BF).ap()
    vc_d = nc.dram_tensor("vcs", [10, CTX, 128], BF).ap()
    CH_ROWS = (1024, 1024, 512)
    kvl_h = [[nc.dram_tensor(f"kvl{t}_{c}", [CH_ROWS[c], TT_], BF) for c in range(3)] for t in range(NT)]
    kva_h = [[nc.dram_tensor(f"kva{t}_{c}", [4 * CH_ROWS[c], TT_], BF) for c in range(3)] for t in range(NT)]

    def L(t, row0):
        c, l0 = row0 // 1024, row0 % 1024
        return kvl_h[t][c].ap()[l0:l0 + 128, :]

    def G(t, r, row0):
        c, l0 = row0 // 1024, row0 % 1024
        b0 = r * CH_ROWS[c] + l0
        return kva_h[t][c].ap()[b0:b0 + 128, :]
    dbg = {}
    if DEBUG_OUT:
        dbg["x1"] = nc.dram_tensor("dbg_x1", [D, NLOC], F32, kind="ExternalOutput").ap()
        dbg["ot"] = nc.dram_tensor("dbg_ot", [16, 128, NLOC], BF, kind="ExternalOutput").ap()
        dbg["q"] = nc.dram_tensor("dbg_q", [16, 128, NLOC], BF, kind="ExternalOutput").ap()
        dbg["kv"] = nc.dram_tensor("dbg_kv", [4 * KVROWS, TT_], BF, kind="ExternalOutput").ap()

    R_x1d = [Res() for _ in range(NT)]
    R_qd = [[Res() for _ in range(NT)] for _ in range(16)]
    R_kcd = [Res() for _ in range(10)]
    R_vcd = [Res() for _ in range(10)]
    R_kvl = [Res() for _ in range(NT)]
    R_kva = [Res() for _ in range(NT)]

    with ExitStack() as es:
        def sb(name, shape, dt):
            return es.enter_context(nc.sbuf_tensor("t_" + name, list(shape), dt))

        banks = [es.enter_context(nc.psum_tensor(f"bank{i}", [128, 512], F32)) for i in range(8)]
        R_bank = [Res(True) for _ in range(8)]

        modT = sb("modT", [128, 144, 2], F32)
        lnp = sb("lnp", [128, 96], F32)
        subg = sb("subg", [128, 1], F32)
        esink = sb("esink", [128, 8], F32)
        lamt = sb("lamt", [128, 4], F32)
        ones_r = sb("ones_r", [128, 128], F32R)
        ones_b = sb("ones_b", [128, 128], BF)
        onesf = sb("onesf", [128, 128], F32)
        epsln = sb("epsln", [128, 2], F32)
        pmat = sb("pmat", [128, 2, 128], F32R)
        mask3 = sb("mask3", [128, 384], BF)
        medge = sb("medge", [128, 8, 128], BF)
        R_const = Res()

        def mod(i, c, n):
            return modT[:, i * 16 + c, n:n + 1]

        def lncol(v, c):
            return lnp[:, v * 16 + c:v * 16 + c + 1]

        with ExitStack() as es0:
            def sb0(name, shape, dt):
                return es0.enter_context(nc.sbuf_tensor("t0_" + name, list(shape), dt))
            cc_t = sb0("cc_t", [128, 32], F32)
            s_t = sb0("s_t", [128, 16, 2], F32R)
            bmod_t = sb0("bmod_t", [128, 144], F32)
            sink_t = sb0("sink_t", [128, 8], F32)
            lamv_t = sb0("lamv_t", [128, 256], F32)
            lamp = sb0("lamp", [128, 128], F32)
            lams = sb0("lams", [128, 4], F32)
            m3f = sb0("m3f", [128, 384], F32)
            mef = sb0("mef", [128, 1024], F32)
            wm = sb0("wm", [128, 2, 16, 512], F32R)
            R_wm = [Res(), Res()]
            modrow = sb0("modrow", [2, 9 * D], F32)
            ident2 = sb0("ident2", [2, 2], F32)
            R_modrow, R_id = Res(), Res()
            R_cc, R_s, R_misc, R_lam = Res(), Res(), Res(), Res()

            S.dma("sp", DMA(cc_t[:], cc_d[:, :]), "ld0", writes=[R_cc])
            S.dma("sp", DMA(bmod_t[:], bmod_d[:, :]), "ld1", writes=[R_misc])
            S.dma("sp", DMA(lnp[:], lnp_d[:, :]), "ld1", writes=[R_misc])
            S.dma("sp", DMA(subg[:], subg_d[:, :]), "ld1", writes=[R_misc])
            S.dma("sp", DMA(sink_t[:], sink_d[:, :]), "ld1", writes=[R_misc])
            S.dma("sp", DMA(lamv_t[:], lamv_d[:, :]), "ld1", writes=[R_misc])
            S.dma("sp", DMA(m3f[:], mask3_d[:, :]), "ld1", writes=[R_misc])
            S.dma("sp", DMA(mef[:], medge_d[:, :]), "ld1", writes=[R_misc])
            S.dma("pool", DMA(pmat[:, 0, :], pmat_d[0]), "ld2", writes=[R_const])
            S.dma("pool", DMA(pmat[:, 1, :], pmat_d[1]), "ld2", writes=[R_const])
            R_onesf = Res()
            S.op("dve", MSET(onesf[:], 1.0), writes=[R_onesf])
            S.op("dve", CP(ones_r[:], onesf[:]), reads=[R_onesf], writes=[R_const])
            S.op("dve", CP(ones_b[:], onesf[:]), reads=[R_onesf], writes=[R_const])
            S.op("dve", MSET(epsln[:, 0:1], LN_EPS), writes=[R_const])
            S.op("dve", MSET(epsln[:, 1:2], RMS_EPS), writes=[R_const])
            S.op("dve", CP(mask3[:], m3f[:]), reads=[R_misc], writes=[R_const])
            S.op("dve", CP(medge[:].rearrange("p a b -> p (a b)"), mef[:]), reads=[R_misc], writes=[R_const])
            S.op("act", ACT(s_t[:].rearrange("p a b -> p (a b)"), cc_t[:], AF.Silu), reads=[R_cc], writes=[R_s])
            S.op("act", ACT(esink[:], sink_t[:], AF.Exp), reads=[R_misc], writes=[R_const])
            S.op("dve", TTo(lamp[:, 0:64], lamv_t[:, 0:64], lamv_t[:, 64:128], ALU.mult), reads=[R_misc], writes=[R_lam])
            S.op("dve", TTo(lamp[:, 64:128], lamv_t[:, 128:192], lamv_t[:, 192:256], ALU.mult), reads=[R_misc, R_lam], writes=[R_lam])
            S.op("dve", lambda e: e.reduce_sum(out=lams[:, 0:1], in_=lamp[:, 0:64], axis=AX.X), reads=[R_lam], writes=[R_lam])
            S.op("dve", lambda e: e.reduce_sum(out=lams[:, 1:2], in_=lamp[:, 64:128], axis=AX.X), reads=[R_lam], writes=[R_lam])
            S.op("act", ACT(lams[:, 2:4], lams[:, 0:2], AF.Exp), reads=[R_lam], writes=[R_lam])
            S.op("dve", TTo(lamt[:, 0:1], lams[:, 2:3], lams[:, 3:4], ALU.subtract), reads=[R_lam], writes=[R_lam])
            S.op("dve", TS(lamt[:, 0:1], lamt[:, 0:1], LAM_INIT, None, ALU.add), reads=[R_lam], writes=[R_lam])
            S.op("dve", TS(lamt[:, 1:2], lamt[:, 0:1], -1.0, None, ALU.mult), reads=[R_lam], writes=[R_const])

            S.dma("sp", DMA(ident2[:], ident_d[:, :]), "ld3", writes=[R_id])
            wsrc = wmod_d.rearrange("(kc p) n -> p kc n", p=128)
            for cb in range(36):
                buf = cb % 2
                S.dma("pool", DMA(wm[:, buf, :, :], wsrc[:, :, cb * 512:(cb + 1) * 512]), f"wm{buf}", writes=[R_wm[buf]])
                S.group("pe", [MM(banks[buf][0:2, :], s_t[:, kc, :], wm[:, buf, kc, :], kc == 0, kc == 15) for kc in range(16)],
                        reads=[R_wm[buf], R_s], writes=[R_bank[buf]])
                S.op("act", ACT(modrow[0:2, cb * 512:(cb + 1) * 512], banks[buf][0:2, :], AF.Copy), reads=[R_bank[buf]], writes=[R_modrow])
            psm = banks[2]
            for j in range(144):
                S.op("pe", MM(psm[:, j * 2:j * 2 + 2], modrow[0:2, j * 128:(j + 1) * 128], ident2[0:2, 0:2], True, True),
                     reads=[R_modrow, R_id], writes=[R_bank[2]])
            S.op("dve", CP(modT[:].rearrange("p a b -> p (a b)"), psm[:, 0:288]), reads=[R_bank[2]], writes=[R_const])
            for n in range(2):
                S.op("dve", TTo(modT[:, :, n], modT[:, :, n], bmod_t[:], ALU.add), reads=[R_misc, R_const], writes=[R_const])
            for i in (1, 4, 7):
                S.op("dve", TS(modT[:, i * 16:(i + 1) * 16, :], modT[:, i * 16:(i + 1) * 16, :], 1.0, None, ALU.add),
                     reads=[R_const], writes=[R_const])
            for i in (2, 8):
                S.op("dve", TS(modT[:, i * 16:(i + 1) * 16, :], modT[:, i * 16:(i + 1) * 16, :], 0.5, None, ALU.mult),
                     reads=[R_const], writes=[R_const])
        S.barrier()


        class FfnCtx:
            pass

        def make_ffn(esx, pf):
            F = FfnCtx()

            def sbx(name, shape, dt):
                return esx.enter_context(nc.sbuf_tensor(pf + name, list(shape), dt))
            F.xt = sbx("xt", [128, DC, TT_], F32)
            F.ht = sbx("ht", [128, DC, TT_], BF)
            F.actt = sbx("actt", [128, HC, TT_], BF)
            NW = 6
            F.wring_t = sbx("wring", [128, NW, D], BF)
            F.wring = Ring(F.wring_t, NW, pf + "wr")
            NWD = 2
            F.wdring_t = sbx("wdring", [128, NWD, HID], BF)
            F.wdring = Ring(F.wdring_t, NWD, pf + "wd")
            F.sgt = sbx("sgt", [128, 2, TT_], F32)
            F.zr = sbx("zr", [128, 2, TT_], F32R)
            F.zsq = sbx("zsq", [128, 2, TT_], F32R)
            F.st_mean = sbx("st_mean", [128, TT_], F32)
            F.st_a = sbx("st_a", [128, TT_], F32)
            F.st_rstd = sbx("st_rstd", [128, TT_], F32)
            F.st_nmr = sbx("st_nmr", [128, TT_], F32)
            F.lt1 = sbx("lt1", [128, 2, TT_], F32)
            F.lt2 = sbx("lt2", [128, 2, TT_], F32)
            F.R_xt = [Res() for _ in range(DC)]
            F.R_ht = [Res() for _ in range(DC)]
            F.R_act = [Res() for _ in range(HC)]
            F.R_sg = [Res(), Res()]
            F.R_zr = [Res(), Res()]
            F.R_zsq = [Res(), Res()]
            F.R_st = Res()
            F.R_lt1 = [Res(), Res()]
            F.R_lt2 = [Res(), Res()]
            cnt = {"g": 0, "y": 0, "z": 0, "l": 0}
            xt, ht, actt = F.xt, F.ht, F.actt

            def wload(src_ap, width=D):
                k = F.wring.next()
                S.dma("pool", DMA(F.wring_t[:, k, 0:width], src_ap), f"{pf}wr{k}", writes=[F.wring.res[k]])
                return k
            F.wload = wload

            def modulate_to_h(T, i_sc, i_sh, n):
                for c in range(DC):
                    S.op("dve", TS(ht[:, c, :T], xt[:, c, :T], mod(i_sc, c, n), mod(i_sh, c, n), ALU.mult, ALU.add),
                         reads=[F.R_xt[c], R_const], writes=[F.R_ht[c]])
            F.modulate_to_h = modulate_to_h

            def ln_finish(T, v, post=None):
                psS, psQ = banks[6], banks[7]
                S.op("act", ACT(F.st_mean[:, :T], psS[:, :T], AF.Identity, scale=1.0 / D), reads=[R_bank[6]], writes=[F.R_st])
                S.op("dve", TTo(F.st_a[:, :T], F.st_mean[:, :T], F.st_mean[:, :T], ALU.mult), reads=[F.R_st], writes=[F.R_st])
                S.op("dve", STT(F.st_a[:, :T], psQ[:, :T], 1.0 / D, F.st_a[:, :T], ALU.mult, ALU.subtract),
                     reads=[R_bank[7], F.R_st], writes=[F.R_st])
                S.op("act", ACT(F.st_a[:, :T], F.st_a[:, :T], AF.Sqrt, bias=epsln[:, 0:1], scale=1.0),
                     reads=[F.R_st, R_const], writes=[F.R_st])
                S.op("dve", RCP(F.st_rstd[:, :T], F.st_a[:, :T]), reads=[F.R_st], writes=[F.R_st])
                S.op("dve", STT(F.st_nmr[:, :T], F.st_mean[:, :T], -1.0, F.st_rstd[:, :T], ALU.mult, ALU.mult),
                     reads=[F.R_st], writes=[F.R_st])
                for c in range(DC):
                    b = cnt["l"] % 2
                    cnt["l"] += 1
                    S.op("dve", TTo(F.lt1[:, b, :T], xt[:, c, :T], F.st_rstd[:, :T], ALU.mult),
                         reads=[F.R_xt[c], F.R_st], writes=[F.R_lt1[b]])
                    S.op("dve", TTo(F.lt2[:, b, :T], F.lt1[:, b, :T], F.st_nmr[:, :T], ALU.add),
                         reads=[F.R_lt1[b], F.R_st], writes=[F.R_lt2[b]])
                    S.op("act", ACT(xt[:, c, :T], F.lt2[:, b, :T], AF.Identity, bias=lncol(v + 1, c), scale=lncol(v, c)),
                         reads=[F.R_lt2[b], R_const], writes=[F.R_xt[c]])
                    if post is not None:
                        post(c)
            F.ln_finish = ln_finish

            def residual_stats(T, dc, by, gscal):
                S.op("dve", STT(xt[:, dc, :T], banks[by][:, :T], gscal, xt[:, dc, :T], ALU.mult, ALU.add),
                     reads=[R_bank[by], F.R_xt[dc], R_const], writes=[F.R_xt[dc]])
                b = cnt["z"] % 2
                cnt["z"] += 1
                S.op("act", ACT(F.zr[:, b, :T], xt[:, dc, :T], AF.Copy), reads=[F.R_xt[dc]], writes=[F.R_zr[b]])
                S.op("act", ACT(F.zsq[:, b, :T], xt[:, dc, :T], AF.Square), reads=[F.R_xt[dc]], writes=[F.R_zsq[b]])
                S.op("pe", MM(banks[6][:, :T], ones_r[:], F.zr[:, b, :T], dc == 0, dc == DC - 1),
                     reads=[F.R_zr[b], R_const], writes=[R_bank[6]])
                S.op("pe", MM(banks[7][:, :T], ones_r[:], F.zsq[:, b, :T], dc == 0, dc == DC - 1),
                     reads=[F.R_zsq[b], R_const], writes=[R_bank[7]])
            F.residual_stats = residual_stats

            def scale_x(T):
                for c in range(DC):
                    S.op("act", ACT(xt[:, c, :T], xt[:, c, :T], AF.Identity, scale=ALPHA), reads=[F.R_xt[c]], writes=[F.R_xt[c]])
            F.scale_x = scale_x

            def ffn(T, n, i0, wg_d, wu_d, wd_d, v, post=None):
                modulate_to_h(T, i0 + 1, i0, n)
                scale_x(T)
                if KSUB <= 1:
                    return
                for hc in range(HC):
                    kg = wload(wg_d[hc])
                    ku = wload(wu_d[hc])
                    b = cnt["g"] % 2
                    cnt["g"] += 1
                    bg, bu = b, 2 + b
                    S.group("pe", [MM(banks[bg][:, :T], F.wring_t[:, kg, kc * 128:(kc + 1) * 128], ht[:, kc, :T], kc == 0, kc == DC - 1)
                                   for kc in range(DC)], reads=[F.wring.res[kg]] + F.R_ht, writes=[R_bank[bg]])
                    S.group("pe", [MM(banks[bu][:, :T], F.wring_t[:, ku, kc * 128:(kc + 1) * 128], ht[:, kc, :T], kc == 0, kc == DC - 1)
                                   for kc in range(DC)], reads=[F.wring.res[ku]] + F.R_ht, writes=[R_bank[bu]])
                    S.op("act", ACT(F.sgt[:, b, :T], banks[bg][:, :T], AF.Silu), reads=[R_bank[bg]], writes=[F.R_sg[b]])
                    S.op("dve", TTo(actt[:, hc, :T], F.sgt[:, b, :T], banks[bu][:, :T], ALU.mult),
                         reads=[F.R_sg[b], R_bank[bu]], writes=[F.R_act[hc]])
                if KSUB <= 2:
                    return
                for dc in range(DC):
                    k = F.wdring.next()
                    S.dma("pool", DMA(F.wdring_t[:, k, :], wd_d[dc]), f"{pf}wd{k}", writes=[F.wdring.res[k]])
                    by = 4 + cnt["y"] % 2
                    cnt["y"] += 1
                    S.group("pe", [MM(banks[by][:, :T], F.wdring_t[:, k, hc * 128:(hc + 1) * 128], actt[:, hc, :T], hc == 0, hc == HC - 1)
                                   for hc in range(HC)], reads=[F.wdring.res[k]] + F.R_act, writes=[R_bank[by]])
                    residual_stats(T, dc, by, mod(i0 + 2, dc, n))
                if KSUB <= 3:
                    return
                ln_finish(T, v, post)
            F.ffn = ffn
            return F

        with contextlib.suppress(_Skip), ExitStack() as es2:
            if KSTOP < 1:
                raise _Skip()
            F = make_ffn(es2, "a_")
            xt, ht = F.xt, F.ht

            def sb2(name, shape, dt):
                return es2.enter_context(nc.sbuf_tensor("t2_" + name, list(shape), dt))
            ropet = sb2("ropet", [128, 4, TT_], F32)
            qraw = sb2("qraw", [128, 2, TT_], F32R)
            rt1 = sb2("rt1", [128, 2, TT_], F32)
            rt2 = sb2("rt2", [128, 2, TT_], F32)
            outb = sb2("outb", [128, 4, TT_], BF)
            voutb = sb2("voutb", [128, 2, 4, 128], BF)
            R_rope = Res()
            R_qraw = [Res(), Res()]
            R_rt1 = [Res(), Res()]
            R_rt2 = [Res(), Res()]
            R_outb = [Res() for _ in range(4)]
            R_vout = [Res(), Res()]
            ca = {"p": 0, "r": 0, "o": 0, "v": 0}

            def proj_fm(T, oc):
                k = F.wload(win_d[oc])
                bp = ca["p"] % 4
                ca["p"] += 1
                S.group("pe", [MM(banks[bp][:, :T], F.wring_t[:, k, kc * 128:(kc + 1) * 128], ht[:, kc, :T], kc == 0, kc == DC - 1)
                               for kc in range(DC)], reads=[F.wring.res[k]] + F.R_ht, writes=[R_bank[bp]])
                return bp

            def store_fm(T, bo, dst_ap, dst_res):
                S.dma("sp", DMA(dst_ap, outb[:, bo, :T]), f"so{bo}", reads=[R_outb[bo]], writes=dst_res)

            def proj_rope(T, oc, which, dst_ap, dst_res):
                bp = proj_fm(T, oc)
                b = ca["r"] % 2
                ca["r"] += 1
                br = 4 + b
                S.op("act", ACT(qraw[:, b, :T], banks[bp][:, :T], AF.Copy), reads=[R_bank[bp]], writes=[R_qraw[b]])
                S.op("pe", MM(banks[br][:, :T], pmat[:, which, :], qraw[:, b, :T], True, True),
                     reads=[R_qraw[b], R_const], writes=[R_bank[br]])
                S.op("dve", TTo(rt1[:, b, :T], banks[bp][:, :T], ropet[:, 2 * which, :T], ALU.mult),
                     reads=[R_bank[bp], R_rope], writes=[R_rt1[b]])
                S.op("dve", TTo(rt2[:, b, :T], banks[br][:, :T], ropet[:, 2 * which + 1, :T], ALU.mult),
                     reads=[R_bank[br], R_rope], writes=[R_rt2[b]])
                bo = ca["o"] % 4
                ca["o"] += 1
                S.op("dve", TTo(outb[:, bo, :T], rt1[:, b, :T], rt2[:, b, :T], ALU.add),
                     reads=[R_rt1[b], R_rt2[b]], writes=[R_outb[bo]])
                store_fm(T, bo, dst_ap, dst_res)

            def proj_plain(T, oc, dst_ap, dst_res):
                bp = proj_fm(T, oc)
                bo = ca["o"] % 4
                ca["o"] += 1
                S.op("act", ACT(outb[:, bo, :T], banks[bp][:, :T], AF.Copy), reads=[R_bank[bp]], writes=[R_outb[bo]])
                store_fm(T, bo, dst_ap, dst_res)

            def proj_v(T, oc, dst_ap, dst_res):
                k = F.wload(win_d[oc])
                bv = 6 + ca["v"] % 2
                ntb = T // 128
                for tb in range(ntb):
                    S.group("pe", [MM(banks[bv][:, tb * 128:(tb + 1) * 128], ht[:, kc, tb * 128:(tb + 1) * 128],
                                      F.wring_t[:, k, kc * 128:(kc + 1) * 128], kc == 0, kc == DC - 1)
                                   for kc in range(DC)], reads=[F.wring.res[k]] + F.R_ht, writes=[R_bank[bv]])
                b = ca["v"] % 2
                ca["v"] += 1
                S.op("act", ACT(voutb[:, b, 0:ntb, :], banks[bv][:, 0:T].rearrange("p (a b) -> p a b", b=128), AF.Copy),
                     reads=[R_bank[bv]], writes=[R_vout[b]])
                S.dma("sp", DMA(dst_ap.rearrange("(a p) d -> p a d", p=128), voutb[:, b, 0:ntb, :]), f"sv{b}",
                      reads=[R_vout[b]], writes=dst_res)

            def vview(ap2d):
                return ap2d.rearrange("r (x d) -> (r x) d", d=128)

            for tile in KTILES:
                is_ctx = tile == 4
                T = CTX if is_ctx else TT_
                n = 1 if is_ctx else 0
                src = ctxT_d if is_ctx else xT_d
                t0 = 0 if is_ctx else tile * TT_
                S.dma("sp", DMA(xt[:, :, :T], src.rearrange("(c p) t -> p c t", p=128)[:, :, t0:t0 + T]), "ldx", writes=F.R_xt)
                if not is_ctx:
                    S.dma("sp", DMA(ropet[:, 0:2, :], ropeA_d.rearrange("a p t -> p a t")[:, :, t0:t0 + T]), "ldr", writes=[R_rope])
                    S.dma("sp", DMA(ropet[:, 2:4, :], ropeB_d.rearrange("a p t -> p a t")[:, :, t0:t0 + T]), "ldr", writes=[R_rope])
                F.ffn(T, n, 0, wg1_d, wu1_d, wd1_d, 0)
                if KSUB <= 4:
                    continue
                if not is_ctx:
                    S.dma("sp", DMA(x1_d.rearrange("(c p) t -> p c t", p=128)[:, :, t0:t0 + T], xt[:, :, :T]), "stx",
                          reads=F.R_xt, writes=[R_x1d[tile]])
                    if DEBUG_OUT:
                        S.dma("sp", DMA(dbg["x1"].rearrange("(c p) t -> p c t", p=128)[:, :, t0:t0 + T], xt[:, :, :T]), "stdbg",
                              reads=F.R_xt)
                F.modulate_to_h(T, 4, 3, n)
                if KSUB <= 5:
                    continue
                if is_ctx:
                    for j, oc in enumerate([8, 9] + list(range(20, 28))):
                        proj_plain(T, oc, kc_d[j], [R_kcd[j]])
                    for j, oc in enumerate([10, 11] + list(range(28, 36))):
                        proj_v(T, oc, vc_d[j], [R_vcd[j]])
                else:
                    for hq in range(8 if KPROJ & 1 else 0):
                        proj_rope(T, hq, 0, q_d[hq][:, t0:t0 + T], [R_qd[hq][tile]])
                    for g in range(2 if KPROJ & 1 else 0):
                        proj_rope(T, 8 + g, 0, L(tile, R_KA + g * 128), [R_kvl[tile]])
                    for h in range(8 if KPROJ & 1 else 0):
                        proj_rope(T, 12 + h, 1, q_d[8 + h][:, t0:t0 + T], [R_qd[8 + h][tile]])
                    for h in range(8 if KPROJ & 1 else 0):
                        proj_rope(T, 20 + h, 1, L(tile, R_KB + h * 128), [R_kvl[tile]])
                    for g in range(2 if KPROJ & 2 else 0):
                        proj_v(T, 10 + g, vview(L(tile, R_VA + g * 128)), [R_kvl[tile]])
                    for h in range(8 if KPROJ & 2 else 0):
                        proj_v(T, 28 + h, vview(L(tile, R_VB + h * 128)), [R_kvl[tile]])
                    d = {k_: v_ for k_, v_ in S.dcnt.items() if k_.startswith("so") or k_.startswith("sv")}
                    S._waits("pool", d)
                    if not NOCC and KSTOP >= 2:
                        for c3 in range(3):
                            S.dma("pool", (lambda tl, cc_: (lambda e: e.collective_compute(
                                "AllGather", ALU.bypass, replica_groups=[[0, 1, 2, 3], [4, 5, 6, 7]],
                                ins=[kvl_h[tl][cc_].ap().opt()], outs=[kva_h[tl][cc_].ap().opt()])))(tile, c3),
                                f"cc{tile}_{c3}", reads=[R_kvl[tile]], writes=[R_kva[tile]], inc=1)
        S.barrier()

        ot_d = nc.dram_tensor("ots", [16, 128, NLOC], BF).ap()
        R_otd = [[Res() for _ in range(NT)] for _ in range(16)]
        with contextlib.suppress(_Skip), ExitStack() as es3:
            if KSTOP < 3:
                raise _Skip()
            def sb3(name, shape, dt):
                return es3.enter_context(nc.sbuf_tensor("t3_" + name, list(shape), dt))
            NK = 256 + SEQ
            KT = sb3("KT", [128, 2, NK], BF)
            VT = sb3("VT", [128, 2, NKB_B, 128], BF)
            QT = sb3("QT", [128, 2, TT_], BF)
            pT = sb3("pT", [128, 4, TT_], BF)
            af1 = sb3("af1", [128, 2, TT_], F32)
            af2 = sb3("af2", [128, 2, TT_], F32)
            af3 = sb3("af3", [128, TT_], F32)
            osq = sb3("osq", [128, TT_], F32R)
            ob = sb3("ob", [128, 2, TT_], BF)
            lacc = sb3("lacc", [128, 2, TT_], F32)
            R_lacc = Res()
            R_KT = [Res(), Res()]
            R_VT = [Res(), Res()]
            R_QT = [Res(), Res()]
            R_pT = [Res() for _ in range(4)]
            R_af1 = [Res(), Res()]
            R_af2 = [Res(), Res()]
            R_af3 = Res()
            R_osq = Res()
            R_ob = [Res(), Res()]
            aa = {"kv": 0, "q": 0, "s": 0, "p": 0, "o": 0}
            SCALE_A = 128.0 ** -0.5
            SCALE_B = 64.0 ** -0.5

            def load_q(qc, gq):
                b = aa["q"] % 2
                aa["q"] += 1
                S.dma("sp", DMA(QT[:, b, :], q_d[qc][:, gq * TT_:(gq + 1) * TT_]), f"lq{b}",
                      reads=[R_qd[qc][gq]], writes=[R_QT[b]])
                return b

            def store_o(qc, gq, bo):
                S.dma("sp", DMA(ot_d[qc][:, gq * TT_:(gq + 1) * TT_], ob[:, bo, :]), f"sob{bo}",
                      reads=[R_ob[bo]], writes=[R_otd[qc][gq]])
                if DEBUG_OUT:
                    S.dma("sp", DMA(dbg["ot"][qc][:, gq * TT_:(gq + 1) * TT_], ob[:, bo, :]), f"sob{bo}", reads=[R_ob[bo]])

            for g in range(2):
                kb_ = aa["kv"] % 2
                aa["kv"] += 1
                S.dma("sp", DMA(KT[:, kb_, 0:CTX], kc_d[g]), f"lk{kb_}", reads=[R_kcd[g]], writes=[R_KT[kb_]])
                S.dma("sp", DMA(VT[:, kb_, 0:2, :], vc_d[g].rearrange("(a p) d -> p a d", p=128)), f"lv{kb_}",
                      reads=[R_vcd[g]], writes=[R_VT[kb_]])
                for t in range(NT):
                    S.dma("sp", DMA(KT[:, kb_, CTX + t * TT_:CTX + (t + 1) * TT_], L(t, R_KA + g * 128)),
                          f"lk{kb_}", reads=[R_kvl[t]], writes=[R_KT[kb_]])
                    S.dma("sp", DMA(VT[:, kb_, 2 + 4 * t:2 + 4 * (t + 1), :],
                                    vview(L(t, R_VA + g * 128)).rearrange("(a p) d -> p a d", p=128)),
                          f"lv{kb_}", reads=[R_kvl[t]], writes=[R_VT[kb_]])
                for r in range(4):
                    base = r * KVROWS
                    S.dma("sp", DMA(KT[:, kb_, CTX + NLOC + r * 128:CTX + NLOC + (r + 1) * 128],
                                    G(3, r, R_KA + g * 128)[:, 384:512]),
                          f"lk{kb_}", reads=[R_kva[3]], writes=[R_KT[kb_]])
                    S.dma("sp", DMA(KT[:, kb_, CTX + NLOC + (4 + r) * 128:CTX + NLOC + (5 + r) * 128],
                                    G(0, r, R_KA + g * 128)[:, 0:128]),
                          f"lk{kb_}", reads=[R_kva[0]], writes=[R_KT[kb_]])
                    S.dma("sp", DMA(VT[:, kb_, 18 + r, :],
                                    vview(G(3, r, R_VA + g * 128))[384:512, :]),
                          f"lv{kb_}", reads=[R_kva[3]], writes=[R_VT[kb_]])
                    S.dma("sp", DMA(VT[:, kb_, 22 + r, :],
                                    vview(G(0, r, R_VA + g * 128))[0:128, :]),
                          f"lv{kb_}", reads=[R_kva[0]], writes=[R_VT[kb_]])
                for gi in range(4):
                    qc = g * 4 + gi
                    for gq in range(4):
                        qb_ = load_q(qc, gq)
                        n0 = gq * 4
                        items = [(0, 0, TT_, None), (1, 0, TT_, None)]
                        for kbl in range(max(n0 - 1, 0), min(n0 + 4, 15) + 1):
                            qlo = max(kbl - 1, n0)
                            qhi = min(kbl + 1, n0 + 3)
                            mlo = (qlo - (kbl - 1)) * 128
                            mhi = (qhi - (kbl - 1) + 1) * 128
                            items.append((2 + kbl, (qlo - n0) * 128, (qhi - n0 + 1) * 128, mask3[:, mlo:mhi]))
                        if n0 == 0:
                            for r in range(4):
                                items.append((18 + r, 0, 128, medge[:, r, :]))
                        if n0 + 3 == 15:
                            for r in range(4):
                                items.append((22 + r, 384, 512, medge[:, 4 + r, :]))
                        nit = len(items)

                        def a_qk(ii):
                            kblk, c0, c1, msk = items[ii]
                            bs = aa["s"] % 4
                            aa["s"] += 1
                            S.op("pe", MM(banks[bs][:, c0:c1], KT[:, kb_, kblk * 128:(kblk + 1) * 128], QT[:, qb_, c0:c1], True, True),
                                 reads=[R_KT[kb_], R_QT[qb_]], writes=[R_bank[bs]])
                            bp = aa["p"] % 4
                            aa["p"] += 1
                            S.op("act", ACT(pT[:, bp, c0:c1], banks[bs][:, c0:c1], AF.Exp, scale=SCALE_A),
                                 reads=[R_bank[bs]], writes=[R_pT[bp]])
                            if msk is not None:
                                S.op("dve", TTo(pT[:, bp, c0:c1], pT[:, bp, c0:c1], msk, ALU.mult),
                                     reads=[R_pT[bp], R_const], writes=[R_pT[bp]])
                            return bp

                        def a_pv(ii, bp):
                            kblk, c0, c1, msk = items[ii]
                            S.op("pe", MM(banks[4][:, c0:c1], VT[:, kb_, kblk, :], pT[:, bp, c0:c1], ii == 0, ii == nit - 1),
                                 reads=[R_VT[kb_], R_pT[bp]], writes=[R_bank[4]])
                            S.op("pe", MM(banks[6][:, c0:c1], ones_b[:], pT[:, bp, c0:c1], ii == 0, ii == nit - 1),
                                 reads=[R_pT[bp], R_const], writes=[R_bank[6]])

                        pend = [a_qk(0), a_qk(1)]
                        for ii in range(nit):
                            if ii + 2 < nit:
                                pend.append(a_qk(ii + 2))
                            a_pv(ii, pend[ii])
                        b = aa["o"] % 2
                        aa["o"] += 1
                        S.op("dve", TS(af1[:, b, :], banks[6][:, :], esink[:, qc:qc + 1], None, ALU.add),
                             reads=[R_bank[6], R_const], writes=[R_af1[b]])
                        S.op("dve", RCP(af1[:, b, :], af1[:, b, :]), reads=[R_af1[b]], writes=[R_af1[b]])
                        S.op("dve", TTo(ob[:, b, :], banks[4][:, :], af1[:, b, :], ALU.mult),
                             reads=[R_bank[4], R_af1[b]], writes=[R_ob[b]])
                        store_o(qc, gq, b)

            for h in range(8):
                kb_ = aa["kv"] % 2
                aa["kv"] += 1
                S.dma("sp", DMA(KT[:, kb_, 0:CTX], kc_d[2 + h]), f"lk{kb_}", reads=[R_kcd[2 + h]], writes=[R_KT[kb_]])
                S.dma("sp", DMA(VT[:, kb_, 0:2, :], vc_d[2 + h].rearrange("(a p) d -> p a d", p=128)), f"lv{kb_}",
                      reads=[R_vcd[2 + h]], writes=[R_VT[kb_]])
                for r in range(4):
                    base = r * KVROWS
                    for t in range(NT):
                        k0 = CTX + r * NLOC + t * TT_
                        S.dma("sp", DMA(KT[:, kb_, k0:k0 + TT_], G(t, r, R_KB + h * 128)),
                              f"lk{kb_}", reads=[R_kva[t]], writes=[R_KT[kb_]])
                        S.dma("sp", DMA(VT[:, kb_, k0 // 128:k0 // 128 + 4, :],
                                        vview(G(t, r, R_VB + h * 128)).rearrange("(a p) d -> p a d", p=128)),
                              f"lv{kb_}", reads=[R_kva[t]], writes=[R_VT[kb_]])
                qc = 8 + h
                for gq in range(4):
                    qb_ = load_q(qc, gq)
                    def b_qk(kblk):
                        bs0 = (aa["s"] % 2) * 2
                        aa["s"] += 1
                        bs1 = bs0 + 1
                        S.op("pe", MM(banks[bs0][:, :], KT[0:64, kb_, kblk * 128:(kblk + 1) * 128], QT[0:64, qb_, :], True, True),
                             reads=[R_KT[kb_], R_QT[qb_]], writes=[R_bank[bs0]])
                        S.op("pe", MM(banks[bs1][:, :], KT[64:128, kb_, kblk * 128:(kblk + 1) * 128], QT[64:128, qb_, :], True, True),
                             reads=[R_KT[kb_], R_QT[qb_]], writes=[R_bank[bs1]])
                        bp0 = (aa["p"] % 2) * 2
                        aa["p"] += 1
                        bp1 = bp0 + 1
                        S.op("act", ACT(pT[:, bp0, :], banks[bs0][:, :], AF.Exp, scale=SCALE_B), reads=[R_bank[bs0]], writes=[R_pT[bp0]])
                        S.op("act", ACT(pT[:, bp1, :], banks[bs1][:, :], AF.Exp, scale=SCALE_B), reads=[R_bank[bs1]], writes=[R_pT[bp1]])
                        return bp0, bp1

                    def b_pv(kblk, bp0, bp1):
                        first, last = kblk == 0, kblk == NKB_B - 1
                        S.op("pe", MM(banks[4][:, :], VT[:, kb_, kblk, :], pT[:, bp0, :], first, last),
                             reads=[R_VT[kb_], R_pT[bp0]], writes=[R_bank[4]])
                        S.op("pe", MM(banks[5][:, :], VT[:, kb_, kblk, :], pT[:, bp1, :], first, last),
                             reads=[R_VT[kb_], R_pT[bp1]], writes=[R_bank[5]])
                        if first:
                            S.op("dve", CP(lacc[:, :, :], pT[:, bp0:bp0 + 2, :]), reads=[R_pT[bp0], R_pT[bp1]], writes=[R_lacc])
                        else:
                            S.op("dve", TTo(lacc[:, :, :], lacc[:, :, :], pT[:, bp0:bp0 + 2, :], ALU.add),
                                 reads=[R_pT[bp0], R_pT[bp1], R_lacc], writes=[R_lacc])
                        if last:
                            S.op("pe", MM(banks[6][:, :], onesf[:], lacc[:, 0, :], True, True), reads=[R_lacc, R_onesf], writes=[R_bank[6]])
                            S.op("pe", MM(banks[7][:, :], onesf[:], lacc[:, 1, :], True, True), reads=[R_lacc, R_onesf], writes=[R_bank[7]])

                    cur = b_qk(0)
                    for kblk in range(NKB_B):
                        nxt = b_qk(kblk + 1) if kblk + 1 < NKB_B else None
                        b_pv(kblk, *cur)
                        cur = nxt
                    S.op("dve", RCP(af1[:, 0, :], banks[6][:, :]), reads=[R_bank[6]], writes=[R_af1[0]])
                    S.op("dve", RCP(af1[:, 1, :], banks[7][:, :]), reads=[R_bank[7]], writes=[R_af1[1]])
                    S.op("dve", TTo(af2[:, 0, :], banks[4][:, :], af1[:, 0, :], ALU.mult), reads=[R_bank[4], R_af1[0]], writes=[R_af2[0]])
                    S.op("dve", TTo(af2[:, 1, :], banks[5][:, :], af1[:, 1, :], ALU.mult), reads=[R_bank[5], R_af1[1]], writes=[R_af2[1]])
                    S.op("dve", STT(af3[:, :], af2[:, 1, :], lamt[:, 1:2], af2[:, 0, :], ALU.mult, ALU.add),
                         reads=[R_af2[0], R_af2[1], R_const], writes=[R_af3])
                    S.op("act", ACT(osq[:, :], af3[:, :], AF.Square), reads=[R_af3], writes=[R_osq])
                    S.op("pe", MM(banks[0][:, :], ones_r[:], osq[:, :], True, True), reads=[R_osq, R_const], writes=[R_bank[0]])
                    S.op("act", ACT(af1[:, 0, :], banks[0][:, :], AF.Sqrt, bias=epsln[:, 1:2], scale=1.0 / 128.0),
                         reads=[R_bank[0], R_const], writes=[R_af1[0]])
                    S.op("dve", RCP(af1[:, 0, :], af1[:, 0, :]), reads=[R_af1[0]], writes=[R_af1[0]])
                    S.op("dve", TTo(af2[:, 0, :], af3[:, :], af1[:, 0, :], ALU.mult), reads=[R_af3, R_af1[0]], writes=[R_af2[0]])
                    b = aa["o"] % 2
                    aa["o"] += 1
                    S.op("dve", TS(ob[:, b, :], af2[:, 0, :], subg[:, 0:1], 1.0 - LAM_INIT, ALU.mult, ALU.mult),
                         reads=[R_af2[0], R_const], writes=[R_ob[b]])
                    store_o(qc, gq, b)
        S.barrier()

        out_toks = []
        with contextlib.suppress(_Skip), ExitStack() as es4:
            if KSTOP < 4:
                raise _Skip()
            F = make_ffn(es4, "c_")
            xt, ht, actt = F.xt, F.ht, F.actt

            def sb4(name, shape, dt):
                return es4.enter_context(nc.sbuf_tensor("t4_" + name, list(shape), dt))
            sga = sb4("sga", [128, TT_], F32)
            sgb = sb4("sgb", [128, TT_], F32)
            mt1 = sb4("mt1", [128, TT_], F32)
            mt2 = sb4("mt2", [128, TT_], F32)
            R_sga, R_sgb, R_mt1, R_mt2 = Res(), Res(), Res(), Res()
            T = TT_
            for tile in range(NT):
                t0 = tile * TT_
                S.dma("sp", DMA(xt[:, :, :], x1_d.rearrange("(c p) t -> p c t", p=128)[:, :, t0:t0 + T]), "ldx2",
                      reads=[R_x1d[tile]], writes=F.R_xt)
                S.dma("sp", DMA(actt[:, 16:32, :], ot_d.rearrange("c p t -> p c t")[:, :, t0:t0 + T]), "ldo",
                      reads=[R_otd[c][tile] for c in range(16)], writes=F.R_act[16:32])
                F.modulate_to_h(T, 4, 3, 0)
                F.scale_x(T)
                for j in range(DC):
                    kga = F.wload(win_d[36 + j])
                    kgb = F.wload(win_d[52 + j])
                    kba = F.wload(wba_d[j], 1024)
                    kbb = F.wload(wbb_d[j], 1024)
                    S.group("pe", [MM(banks[0][:, :], F.wring_t[:, kga, kc * 128:(kc + 1) * 128], ht[:, kc, :], kc == 0, kc == DC - 1)
                                   for kc in range(DC)], reads=[F.wring.res[kga]] + F.R_ht, writes=[R_bank[0]])
                    S.group("pe", [MM(banks[1][:, :], F.wring_t[:, kgb, kc * 128:(kc + 1) * 128], ht[:, kc, :], kc == 0, kc == DC - 1)
                                   for kc in range(DC)], reads=[F.wring.res[kgb]] + F.R_ht, writes=[R_bank[1]])
                    S.group("pe", [MM(banks[2][:, :], F.wring_t[:, kba, kc * 128:(kc + 1) * 128], actt[:, 16 + kc, :], kc == 0, kc == 7)
                                   for kc in range(8)], reads=[F.wring.res[kba]] + F.R_act[16:24], writes=[R_bank[2]])
                    S.group("pe", [MM(banks[3][:, :], F.wring_t[:, kbb, kc * 128:(kc + 1) * 128], actt[:, 24 + kc, :], kc == 0, kc == 7)
                                   for kc in range(8)], reads=[F.wring.res[kbb]] + F.R_act[24:32], writes=[R_bank[3]])
                    S.op("act", ACT(sga[:, :], banks[0][:, :], AF.Sigmoid), reads=[R_bank[0]], writes=[R_sga])
                    S.op("act", ACT(sgb[:, :], banks[1][:, :], AF.Sigmoid), reads=[R_bank[1]], writes=[R_sgb])
                    S.op("dve", TTo(mt1[:, :], sga[:, :], banks[2][:, :], ALU.mult), reads=[R_sga, R_bank[2]], writes=[R_mt1])
                    S.op("dve", TTo(mt2[:, :], sgb[:, :], banks[3][:, :], ALU.mult), reads=[R_sgb, R_bank[3]], writes=[R_mt2])
                    S.op("dve", TTo(actt[:, j, :], mt1[:, :], mt2[:, :], ALU.add), reads=[R_mt1, R_mt2], writes=[F.R_act[j]])
                for dc in range(DC):
                    k = F.wload(wout_d[dc])
                    by = 4 + dc % 2
                    S.group("pe", [MM(banks[by][:, :], F.wring_t[:, k, kc * 128:(kc + 1) * 128], actt[:, kc, :], kc == 0, kc == DC - 1)
                                   for kc in range(DC)], reads=[F.wring.res[k]] + F.R_act[0:16], writes=[R_bank[by]])
                    F.residual_stats(T, dc, by, mod(5, dc, 0))
                F.ln_finish(T, 2)
                F.ffn(T, 0, 6, wg2_d, wu2_d, wd2_d, 4)
                out_toks.append(S.dma("sp", DMA(outT_d.rearrange("(c p) t -> p c t", p=128)[:, :, t0:t0 + T], xt[:, :, :]), "sto",
                                      reads=F.R_xt))
        if DEBUG_OUT and KSTOP >= 1:
            S.barrier()
            S.dma("sp", DMA(dbg["q"][:, :, 0:TT_], q_d[:, :, 0:TT_]), "stdbg")
        if KSTOP < 4:
            S.barrier()
            srcd = (x1_d if KSUB >= 5 else xT_d) if KSTOP >= 1 else wmod_d[0:D, 0:NLOC]
            if len(KTILES) < 5 and KSTOP >= 1:
                S.dma("sp", DMA(outT_d[:, 0:TT_], srcd[:, 0:TT_]), "sto")
            else:
                S.dma("sp", DMA(outT_d[:, :], srcd[:, :] if KSTOP >= 1 else srcd), "sto")
        S.barrier()

        sem_names = list(ENGS) + sorted(S.dcnt.keys())
        sems = {nm: es.enter_context(nc.semaphore("s_" + nm)) for nm in sem_names}
        block = es.enter_context(nc.Block())

        def replay(eng_name):
            def body(e):
                for it in S.prog[eng_name]:
                    if it[0] == "w":
                        e.wait_ge(sems[it[1]], it[2])
                    else:
                        ins = it[1](e)
                        if it[2] is not None:
                            if it[2].startswith("cc"):
                                ins.then_inc(sems[it[2]])
                            else:
                                ins.then_inc(sems[it[2]], it[3])
            return body
        block.tensor(replay("pe"))
        block.scalar(replay("act"))
        block.vector(replay("dve"))
        block.gpsimd(replay("pool"))
        block.sync(replay("sp"))
    return nc


def _tile_w(W):
    K, N = W.shape
    return np.ascontiguousarray(W.reshape(K // 128, 128, N // 128, 128).transpose(2, 1, 0, 3).reshape(N // 128, 128, K))


def _cols(v):
    return np.ascontiguousarray(np.asarray(v, np.float32).reshape(-1, 128).T)


def _rope_tables(pos, dim):
    quarter = dim // 4
    inv = (10000.0 ** (-np.arange(quarter, dtype=np.float32) / quarter)).astype(np.float32)
    row = (pos // 64).astype(np.float32)
    col = (pos % 64).astype(np.float32)
    ang = np.concatenate([row[:, None] * inv, col[:, None] * inv], axis=-1).astype(np.float32)
    cos, sin = np.cos(ang).astype(np.float32), np.sin(ang).astype(np.float32)
    half = dim // 2
    idx = np.arange(128) % half
    return np.ascontiguousarray(np.stack([cos[:, idx].T, sin[:, idx].T]).astype(np.float32))


def _pmat(dim):
    half = dim // 2
    M = np.zeros((128, 128), np.float32)
    for p in range(128):
        i = p % dim
        if i < half:
            M[p, p + half] = -1.0
        else:
            M[p, p - half] = 1.0
    return np.ascontiguousarray(M.T)


_CACHE = {}


def kernel(x, c, ctx, c_ctx, w_mod, b_mod, ffn1_w_gate, ffn1_w_up, ffn1_w_down, ln1_g, ln1_b,
           w_in, a_sink, b_lambda_q1, b_lambda_k1, b_lambda_q2, b_lambda_k2, b_subln_g,
           w_branch_a, w_branch_b, w_out, ln2_g, ln2_b, ffn2_w_gate, ffn2_w_up, ffn2_w_down,
           ln3_g, ln3_b):
    f = lambda a: np.asarray(a, np.float32)
    x, c, ctx, c_ctx = f(x), f(c), f(ctx), f(c_ctx)
    shared = {
        "bmodc": _cols(f(b_mod)[0]),
        "lnp": np.ascontiguousarray(np.concatenate([_cols(f(v)[0]) for v in (ln1_g, ln1_b, ln2_g, ln2_b, ln3_g, ln3_b)], axis=1)),
        "subg": np.ascontiguousarray(f(b_subln_g)[0].reshape(128, 1)),
        "sinkb": np.ascontiguousarray(np.broadcast_to(f(a_sink)[0][None, :], (128, 8))),
        "lamv": np.ascontiguousarray(np.broadcast_to(np.concatenate(
            [f(b_lambda_q1)[0], f(b_lambda_k1)[0], f(b_lambda_q2)[0], f(b_lambda_k2)[0]])[None, :], (128, 256))),
        "w_mod": np.ascontiguousarray(f(w_mod)[0]),
        "wg1": _tile_w(f(ffn1_w_gate)[0]), "wu1": _tile_w(f(ffn1_w_up)[0]), "wd1": _tile_w(f(ffn1_w_down)[0]),
        "wg2": _tile_w(f(ffn2_w_gate)[0]), "wu2": _tile_w(f(ffn2_w_up)[0]), "wd2": _tile_w(f(ffn2_w_down)[0]),
        "w_in": _tile_w(f(w_in)[0]), "w_ba": _tile_w(f(w_branch_a)[0]), "w_bb": _tile_w(f(w_branch_b)[0]),
        "w_out": _tile_w(f(w_out)[0]),
        "pmat": np.ascontiguousarray(np.stack([_pmat(128), _pmat(64)])),
        "ident": np.eye(2, dtype=np.float32),
    }
    jj, ii = np.meshgrid(np.arange(128), np.arange(128), indexing="ij")
    tri_le = (jj <= ii).astype(np.float32)
    tri_ge = (jj >= ii).astype(np.float32)
    shared["mask3"] = np.ascontiguousarray(np.concatenate([tri_le, np.ones((128, 128), np.float32), tri_ge], axis=1))
    in_maps = []
    for r in range(8):
        b, qd = r // 4, r % 4
        m = dict(shared)
        m["xT"] = np.ascontiguousarray(x[b, qd * NLOC:(qd + 1) * NLOC, :].T)
        m["ctxT"] = np.ascontiguousarray(ctx[b].T)
        m["cc"] = np.ascontiguousarray(np.stack([_cols(c[b]), _cols(c_ctx)], axis=2).reshape(128, 32))
        pos = np.arange(qd * NLOC, (qd + 1) * NLOC)
        m["ropeA"] = _rope_tables(pos, 128)
        m["ropeB"] = _rope_tables(pos, 64)
        me = np.zeros((128, 8, 128), np.float32)
        if qd > 0:
            me[:, qd - 1, :] = tri_ge
        if qd < 3:
            me[:, 4 + qd + 1, :] = tri_le
        m["medge"] = np.ascontiguousarray(me.reshape(128, 1024))
        in_maps.append(m)
    if os.environ.get("KSIM"):
        _CACHE["in_maps"] = in_maps
        return None
    if "nc" not in _CACHE:
        _CACHE["nc"] = build_program()
    ncores = int(os.environ.get("KCORES", "8"))
    res = run_bass_kernel_spmd(_CACHE["nc"], in_maps[:ncores], core_ids=list(range(ncores)))
    _CACHE["last"] = res
    out = np.empty((2, SEQ, D), np.float32)
    for r in range(ncores):
        b, qd = r // 4, r % 4
        out[b, qd * NLOC:(qd + 1) * NLOC, :] = np.asarray(res.results[r]["outT"], np.float32).T
    return out
```

```python
import numpy as np
from contextlib import ExitStack
import concourse.bass as bass
import concourse.mybir as mybir
from concourse.bass_utils import run_bass_kernel_spmd

F32 = mybir.dt.float32
F32R = mybir.dt.float32r
BF = mybir.dt.bfloat16
AF = mybir.ActivationFunctionType
ALU = mybir.AluOpType
AX = mybir.AxisListType

D = 2048
DC = 16
HID = 5632
HC = 44
SEQ = 8192
NLOC = 2048
CTX = 256
TT_ = 512
NT = 4
ALPHA = 2.0 ** 0.25
LN_EPS = 1e-5
RMS_EPS = 1e-5
LAM_INIT = 0.2
KVROWS = 2560
R_KA, R_KB, R_VA, R_VB = 0, 256, 1280, 1536
NKB_B = 66
ENGS = ("pe", "act", "dve", "pool", "sp")

import os
import contextlib
DEBUG_OUT = bool(int(os.environ.get("KDEBUG", "0")))
KSTOP = int(os.environ.get("KSTOP", "4"))
NOCC = int(os.environ.get("NOCC", "0"))
KSUB = int(os.environ.get("KSUB", "9"))
KPROJ = int(os.environ.get("KPROJ", "3"))
KTILES = tuple(int(v) for v in os.environ.get("KTILES", "4,0,1,2,3").split(","))


class _Skip(Exception):
    pass


class Res:
    __slots__ = ("w", "r", "x")

    def __init__(self, excl=False):
        self.w = None
        self.r = {}
        self.x = excl


class Sched:
    def __init__(self):
        self.prog = {e: [] for e in ENGS}
        self.cnt = {e: 0 for e in ENGS}
        self.waited = {e: {} for e in ENGS}
        self.dcnt = {}

    def _deps(self, reads, writes):
        d = {}

        def add(k, v):
            if d.get(k, 0) < v:
                d[k] = v
        for r in reads:
            if r.w is not None:
                add(*r.w)
            if r.x:
                for k, v in r.r.items():
                    add(k, v)
        for w in writes:
            if w.w is not None:
                add(*w.w)
            for k, v in w.r.items():
                add(k, v)
        return d

    def _waits(self, eng, d):
        for sem, val in d.items():
            if sem == eng and eng == "pe":
                continue
            if self.waited[eng].get(sem, 0) >= val:
                continue
            self.waited[eng][sem] = val
            self.prog[eng].append(("w", sem, val))

    def group(self, eng, fns, reads=(), writes=()):
        self._waits(eng, self._deps(reads, writes))
        self.cnt[eng] += 1
        tok = (eng, self.cnt[eng])
        for f in fns[:-1]:
            self.prog[eng].append(("o", f, None, 0))
        self.prog[eng].append(("o", fns[-1], eng, 1))
        for r in reads:
            if r.r.get(eng, 0) < tok[1]:
                r.r[eng] = tok[1]
        for w in writes:
            w.w = tok
            w.r = {}
        return tok

    def op(self, eng, fn, reads=(), writes=()):
        return self.group(eng, [fn], reads, writes)

    def dma(self, q, fn, sem, reads=(), writes=(), inc=16):
        self._waits(q, self._deps(reads, writes))
        self.dcnt[sem] = self.dcnt.get(sem, 0) + inc
        tok = (sem, self.dcnt[sem])
        self.prog[q].append(("o", fn, sem, inc))
        for r in reads:
            if r.r.get(sem, 0) < tok[1]:
                r.r[sem] = tok[1]
        for w in writes:
            w.w = tok
            w.r = {}
        return tok

    def barrier(self):
        allv = dict(self.cnt)
        allv.update(self.dcnt)
        for e in ENGS:
            d = {k: v for k, v in allv.items() if v > 0 and k != e}
            self._waits(e, d)


def MM(out, lhsT, rhs, start, stop):
    return lambda e: e.matmul(out, lhsT, rhs, start=start, stop=stop)


def ACT(out, in_, func, **kw):
    return lambda e: e.activation(out=out, in_=in_, func=func, **kw)


def TS(out, in0, s1, s2, op0, op1=None):
    if op1 is None:
        return lambda e: e.tensor_scalar(out=out, in0=in0, scalar1=s1, scalar2=None, op0=op0)
    return lambda e: e.tensor_scalar(out=out, in0=in0, scalar1=s1, scalar2=s2, op0=op0, op1=op1)


def TTo(out, in0, in1, op):
    return lambda e: e.tensor_tensor(out=out, in0=in0, in1=in1, op=op)


def STT(out, in0, scalar, in1, op0, op1):
    return lambda e: e.scalar_tensor_tensor(out=out, in0=in0, scalar=scalar, in1=in1, op0=op0, op1=op1)


def DMA(out, in_):
    return lambda e: e.dma_start(out=out, in_=in_)


def CP(out, in_):
    return lambda e: e.tensor_copy(out=out, in_=in_)


def RCP(out, in_):
    return lambda e: e.reciprocal(out=out, in_=in_)


def MSET(ap, v):
    return lambda e: e.memset(ap, v)


class Ring:
    def __init__(self, tile, n, name):
        self.tile = tile
        self.n = n
        self.i = 0
        self.name = name
        self.res = [Res() for _ in range(n)]

    def next(self):
        k = self.i % self.n
        self.i += 1
        return k


def build_program():
    nc = bass.Bass("TRN2", target_bir_lowering=False)
    S = Sched()

    def din(name, shape, dt=F32):
        return nc.dram_tensor(name, list(shape), dt, kind="ExternalInput").ap()

    xT_d = din("xT", [D, NLOC])
    ctxT_d = din("ctxT", [D, CTX])
    cc_d = din("cc", [128, 32])
    bmod_d = din("bmodc", [128, 144])
    lnp_d = din("lnp", [128, 96])
    subg_d = din("subg", [128, 1])
    sink_d = din("sinkb", [128, 8])
    lamv_d = din("lamv", [128, 256])
    wmod_d = din("w_mod", [D, 9 * D])
    wg1_d = din("wg1", [HC, 128, D])
    wu1_d = din("wu1", [HC, 128, D])
    wd1_d = din("wd1", [DC, 128, HID])
    wg2_d = din("wg2", [HC, 128, D])
    wu2_d = din("wu2", [HC, 128, D])
    wd2_d = din("wd2", [DC, 128, HID])
    win_d = din("w_in", [68, 128, D])
    wba_d = din("w_ba", [DC, 128, 1024])
    wbb_d = din("w_bb", [DC, 128, 1024])
    wout_d = din("w_out", [DC, 128, D])
    ropeA_d = din("ropeA", [2, 128, NLOC])
    ropeB_d = din("ropeB", [2, 128, NLOC])
    pmat_d = din("pmat", [2, 128, 128])
    mask3_d = din("mask3", [128, 384])
    medge_d = din("medge", [128, 8 * 128])
    ident_d = din("ident", [2, 2])
    outT_d = nc.dram_tensor("outT", [D, NLOC], F32, kind="ExternalOutput").ap()

    x1_d = nc.dram_tensor("x1s", [D, NLOC], F32).ap()
    q_d = nc.dram_tensor("qs", [16, 128, NLOC], BF).ap()
    kc_d = nc.dram_tensor("kcs", [10, 128, CTX], BF).ap()
    vc_d = nc.dram_tensor("vcs", [10, CTX, 128], BF).ap()
    CH_ROWS = (1024, 1024, 512)
    kvl_h = [[nc.dram_tensor(f"kvl{t}_{c}", [CH_ROWS[c], TT_], BF) for c in range(3)] for t in range(NT)]
    kva_h = [[nc.dram_tensor(f"kva{t}_{c}", [4 * CH_ROWS[c], TT_], BF) for c in range(3)] for t in range(NT)]

    def L(t, row0):
        c, l0 = row0 // 1024, row0 % 1024
        return kvl_h[t][c].ap()[l0:l0 + 128, :]

    def G(t, r, row0):
        c, l0 = row0 // 1024, row0 % 1024
        b0 = r * CH_ROWS[c] + l0
        return kva_h[t][c].ap()[b0:b0 + 128, :]
    dbg = {}
    if DEBUG_OUT:
        dbg["x1"] = nc.dram_tensor("dbg_x1", [D, NLOC], F32, kind="ExternalOutput").ap()
        dbg["ot"] = nc.dram_tensor("dbg_ot", [16, 128, NLOC], BF, kind="ExternalOutput").ap()
        dbg["q"] = nc.dram_tensor("dbg_q", [16, 128, NLOC], BF, kind="ExternalOutput").ap()
        dbg["kv"] = nc.dram_tensor("dbg_kv", [4 * KVROWS, TT_], BF, kind="ExternalOutput").ap()

    R_x1d = [Res() for _ in range(NT)]
    R_qd = [[Res() for _ in range(NT)] for _ in range(16)]
    R_kcd = [Res() for _ in range(10)]
    R_vcd = [Res() for _ in range(10)]
    R_kvl = [Res() for _ in range(NT)]
    R_kva = [Res() for _ in range(NT)]

    with ExitStack() as es:
        def sb(name, shape, dt):
            return es.enter_context(nc.sbuf_tensor("t_" + name, list(shape), dt))

        banks = [es.enter_context(nc.psum_tensor(f"bank{i}", [128, 512], F32)) for i in range(8)]
        R_bank = [Res(True) for _ in range(8)]

        modT = sb("modT", [128, 144, 2], F32)
        lnp = sb("lnp", [128, 96], F32)
        subg = sb("subg", [128, 1], F32)
        esink = sb("esink", [128, 8], F32)
        lamt = sb("lamt", [128, 4], F32)
        ones_r = sb("ones_r", [128, 128], F32R)
        ones_b = sb("ones_b", [128, 128], BF)
        onesf = sb("onesf", [128, 128], F32)
        epsln = sb("epsln", [128, 2], F32)
        pmat = sb("pmat", [128, 2, 128], F32R)
        mask3 = sb("mask3", [128, 384], BF)
        medge = sb("medge", [128, 8, 128], BF)
        R_const = Res()

        def mod(i, c, n):
            return modT[:, i * 16 + c, n:n + 1]

        def lncol(v, c):
            return lnp[:, v * 16 + c:v * 16 + c + 1]

        with ExitStack() as es0:
            def sb0(name, shape, dt):
                return es0.enter_context(nc.sbuf_tensor("t0_" + name, list(shape), dt))
            cc_t = sb0("cc_t", [128, 32], F32)
            s_t = sb0("s_t", [128, 16, 2], F32R)
            bmod_t = sb0("bmod_t", [128, 144], F32)
            sink_t = sb0("sink_t", [128, 8], F32)
            lamv_t = sb0("lamv_t", [128, 256], F32)
            lamp = sb0("lamp", [128, 128], F32)
            lams = sb0("lams", [128, 4], F32)
            m3f = sb0("m3f", [128, 384], F32)
            mef = sb0("mef", [128, 1024], F32)
            wm = sb0("wm", [128, 2, 16, 512], F32R)
            R_wm = [Res(), Res()]
            modrow = sb0("modrow", [2, 9 * D], F32)
            ident2 = sb0("ident2", [2, 2], F32)
            R_modrow, R_id = Res(), Res()
            R_cc, R_s, R_misc, R_lam = Res(), Res(), Res(), Res()

            S.dma("sp", DMA(cc_t[:], cc_d[:, :]), "ld0", writes=[R_cc])
            S.dma("sp", DMA(bmod_t[:], bmod_d[:, :]), "ld1", writes=[R_misc])
            S.dma("sp", DMA(lnp[:], lnp_d[:, :]), "ld1", writes=[R_misc])
            S.dma("sp", DMA(subg[:], subg_d[:, :]), "ld1", writes=[R_misc])
            S.dma("sp", DMA(sink_t[:], sink_d[:, :]), "ld1", writes=[R_misc])
            S.dma("sp", DMA(lamv_t[:], lamv_d[:, :]), "ld1", writes=[R_misc])
            S.dma("sp", DMA(m3f[:], mask3_d[:, :]), "ld1", writes=[R_misc])
            S.dma("sp", DMA(mef[:], medge_d[:, :]), "ld1", writes=[R_misc])
            S.dma("pool", DMA(pmat[:, 0, :], pmat_d[0]), "ld2", writes=[R_const])
            S.dma("pool", DMA(pmat[:, 1, :], pmat_d[1]), "ld2", writes=[R_const])
            R_onesf = Res()
            S.op("dve", MSET(onesf[:], 1.0), writes=[R_onesf])
            S.op("dve", CP(ones_r[:], onesf[:]), reads=[R_onesf], writes=[R_const])
            S.op("dve", CP(ones_b[:], onesf[:]), reads=[R_onesf], writes=[R_const])
            S.op("dve", MSET(epsln[:, 0:1], LN_EPS), writes=[R_const])
            S.op("dve", MSET(epsln[:, 1:2], RMS_EPS), writes=[R_const])
            S.op("dve", CP(mask3[:], m3f[:]), reads=[R_misc], writes=[R_const])
            S.op("dve", CP(medge[:].rearrange("p a b -> p (a b)"), mef[:]), reads=[R_misc], writes=[R_const])
            S.op("act", ACT(s_t[:].rearrange("p a b -> p (a b)"), cc_t[:], AF.Silu), reads=[R_cc], writes=[R_s])
            S.op("act", ACT(esink[:], sink_t[:], AF.Exp), reads=[R_misc], writes=[R_const])
            S.op("dve", TTo(lamp[:, 0:64], lamv_t[:, 0:64], lamv_t[:, 64:128], ALU.mult), reads=[R_misc], writes=[R_lam])
            S.op("dve", TTo(lamp[:, 64:128], lamv_t[:, 128:192], lamv_t[:, 192:256], ALU.mult), reads=[R_misc, R_lam], writes=[R_lam])
            S.op("dve", lambda e: e.reduce_sum(out=lams[:, 0:1], in_=lamp[:, 0:64], axis=AX.X), reads=[R_lam], writes=[R_lam])
            S.op("dve", lambda e: e.reduce_sum(out=lams[:, 1:2], in_=lamp[:, 64:128], axis=AX.X), reads=[R_lam], writes=[R_lam])
            S.op("act", ACT(lams[:, 2:4], lams[:, 0:2], AF.Exp), reads=[R_lam], writes=[R_lam])
            S.op("dve", TTo(lamt[:, 0:1], lams[:, 2:3], lams[:, 3:4], ALU.subtract), reads=[R_lam], writes=[R_lam])
            S.op("dve", TS(lamt[:, 0:1], lamt[:, 0:1], LAM_INIT, None, ALU.add), reads=[R_lam], writes=[R_lam])
            S.op("dve", TS(lamt[:, 1:2], lamt[:, 0:1], -1.0, None, ALU.mult), reads=[R_lam], writes=[R_const])

            S.dma("sp", DMA(ident2[:], ident_d[:, :]), "ld3", writes=[R_id])
            wsrc = wmod_d.rearrange("(kc p) n -> p kc n", p=128)
            for cb in range(36):
                buf = cb % 2
                S.dma("pool", DMA(wm[:, buf, :, :], wsrc[:, :, cb * 512:(cb + 1) * 512]), f"wm{buf}", writes=[R_wm[buf]])
                S.group("pe", [MM(banks[buf][0:2, :], s_t[:, kc, :], wm[:, buf, kc, :], kc == 0, kc == 15) for kc in range(16)],
                        reads=[R_wm[buf], R_s], writes=[R_bank[buf]])
                S.op("act", ACT(modrow[0:2, cb * 512:(cb + 1) * 512], banks[buf][0:2, :], AF.Copy), reads=[R_bank[buf]], writes=[R_modrow])
            psm = banks[2]
            for j in range(144):
                S.op("pe", MM(psm[:, j * 2:j * 2 + 2], modrow[0:2, j * 128:(j + 1) * 128], ident2[0:2, 0:2], True, True),
                     reads=[R_modrow, R_id], writes=[R_bank[2]])
            S.op("dve", CP(modT[:].rearrange("p a b -> p (a b)"), psm[:, 0:288]), reads=[R_bank[2]], writes=[R_const])
            for n in range(2):
                S.op("dve", TTo(modT[:, :, n], modT[:, :, n], bmod_t[:], ALU.add), reads=[R_misc, R_const], writes=[R_const])
            for i in (1, 4, 7):
                S.op("dve", TS(modT[:, i * 16:(i + 1) * 16, :], modT[:, i * 16:(i + 1) * 16, :], 1.0, None, ALU.add),
                     reads=[R_const], writes=[R_const])
            for i in (2, 8):
                S.op("dve", TS(modT[:, i * 16:(i + 1) * 16, :], modT[:, i * 16:(i + 1) * 16, :], 0.5, None, ALU.mult),
                     reads=[R_const], writes=[R_const])
        S.barrier()


        class FfnCtx:
            pass

        def make_ffn(esx, pf):
            F = FfnCtx()

            def sbx(name, shape, dt):
                return esx.enter_context(nc.sbuf_tensor(pf + name, list(shape), dt))
            F.xt = sbx("xt", [128, DC, TT_], F32)
            F.ht = sbx("ht", [128, DC, TT_], BF)
            F.actt = sbx("actt", [128, HC, TT_], BF)
            NW = 6
            F.wring_t = sbx("wring", [128, NW, D], BF)
            F.wring = Ring(F.wring_t, NW, pf + "wr")
            NWD = 2
            F.wdring_t = sbx("wdring", [128, NWD, HID], BF)
            F.wdring = Ring(F.wdring_t, NWD, pf + "wd")
            F.sgt = sbx("sgt", [128, 2, TT_], F32)
            F.zr = sbx("zr", [128, 2, TT_], F32R)
            F.zsq = sbx("zsq", [128, 2, TT_], F32R)
            F.st_mean = sbx("st_mean", [128, TT_], F32)
            F.st_a = sbx("st_a", [128, TT_], F32)
            F.st_rstd = sbx("st_rstd", [128, TT_], F32)
            F.st_nmr = sbx("st_nmr", [128, TT_], F32)
            F.lt1 = sbx("lt1", [128, 2, TT_], F32)
            F.lt2 = sbx("lt2", [128, 2, TT_], F32)
            F.R_xt = [Res() for _ in range(DC)]
            F.R_ht = [Res() for _ in range(DC)]
            F.R_act = [Res() for _ in range(HC)]
            F.R_sg = [Res(), Res()]
            F.R_zr = [Res(), Res()]
            F.R_zsq = [Res(), Res()]
            F.R_st = Res()
            F.R_lt1 = [Res(), Res()]
            F.R_lt2 = [Res(), Res()]
            cnt = {"g": 0, "y": 0, "z": 0, "l": 0}
            xt, ht, actt = F.xt, F.ht, F.actt

            def wload(src_ap, width=D):
                k = F.wring.next()
                S.dma("pool", DMA(F.wring_t[:, k, 0:width], src_ap), f"{pf}wr{k}", writes=[F.wring.res[k]])
                return k
            F.wload = wload

            def modulate_to_h(T, i_sc, i_sh, n):
                for c in range(DC):
                    S.op("dve", TS(ht[:, c, :T], xt[:, c, :T], mod(i_sc, c, n), mod(i_sh, c, n), ALU.mult, ALU.add),
                         reads=[F.R_xt[c], R_const], writes=[F.R_ht[c]])
            F.modulate_to_h = modulate_to_h

            def ln_finish(T, v, post=None):
                psS, psQ = banks[6], banks[7]
                S.op("act", ACT(F.st_mean[:, :T], psS[:, :T], AF.Identity, scale=1.0 / D), reads=[R_bank[6]], writes=[F.R_st])
                S.op("dve", TTo(F.st_a[:, :T], F.st_mean[:, :T], F.st_mean[:, :T], ALU.mult), reads=[F.R_st], writes=[F.R_st])
                S.op("dve", STT(F.st_a[:, :T], psQ[:, :T], 1.0 / D, F.st_a[:, :T], ALU.mult, ALU.subtract),
                     reads=[R_bank[7], F.R_st], writes=[F.R_st])
                S.op("act", ACT(F.st_a[:, :T], F.st_a[:, :T], AF.Sqrt, bias=epsln[:, 0:1], scale=1.0),
                     reads=[F.R_st, R_const], writes=[F.R_st])
                S.op("dve", RCP(F.st_rstd[:, :T], F.st_a[:, :T]), reads=[F.R_st], writes=[F.R_st])
                S.op("dve", STT(F.st_nmr[:, :T], F.st_mean[:, :T], -1.0, F.st_rstd[:, :T], ALU.mult, ALU.mult),
                     reads=[F.R_st], writes=[F.R_st])
                for c in range(DC):
                    b = cnt["l"] % 2
                    cnt["l"] += 1
                    S.op("dve", TTo(F.lt1[:, b, :T], xt[:, c, :T], F.st_rstd[:, :T], ALU.mult),
                         reads=[F.R_xt[c], F.R_st], writes=[F.R_lt1[b]])
                    S.op("dve", TTo(F.lt2[:, b, :T], F.lt1[:, b, :T], F.st_nmr[:, :T], ALU.add),
                         reads=[F.R_lt1[b], F.R_st], writes=[F.R_lt2[b]])
                    S.op("act", ACT(xt[:, c, :T], F.lt2[:, b, :T], AF.Identity, bias=lncol(v + 1, c), scale=lncol(v, c)),
                         reads=[F.R_lt2[b], R_const], writes=[F.R_xt[c]])
                    if post is not None:
                        post(c)
            F.ln_finish = ln_finish

            def residual_stats(T, dc, by, gscal):
                S.op("dve", STT(xt[:, dc, :T], banks[by][:, :T], gscal, xt[:, dc, :T], ALU.mult, ALU.add),
                     reads=[R_bank[by], F.R_xt[dc], R_const], writes=[F.R_xt[dc]])
                b = cnt["z"] % 2
                cnt["z"] += 1
                S.op("act", ACT(F.zr[:, b, :T], xt[:, dc, :T], AF.Copy), reads=[F.R_xt[dc]], writes=[F.R_zr[b]])
                S.op("act", ACT(F.zsq[:, b, :T], xt[:, dc, :T], AF.Square), reads=[F.R_xt[dc]], writes=[F.R_zsq[b]])
                S.op("pe", MM(banks[6][:, :T], ones_r[:], F.zr[:, b, :T], dc == 0, dc == DC - 1),
                     reads=[F.R_zr[b], R_const], writes=[R_bank[6]])
                S.op("pe", MM(banks[7][:, :T], ones_r[:], F.zsq[:, b, :T], dc == 0, dc == DC - 1),
                     reads=[F.R_zsq[b], R_const], writes=[R_bank[7]])
            F.residual_stats = residual_stats

            def scale_x(T):
                for c in range(DC):
                    S.op("act", ACT(xt[:, c, :T], xt[:, c, :T], AF.Identity, scale=ALPHA), reads=[F.R_xt[c]], writes=[F.R_xt[c]])
            F.scale_x = scale_x

            def ffn(T, n, i0, wg_d, wu_d, wd_d, v, post=None):
                modulate_to_h(T, i0 + 1, i0, n)
                scale_x(T)
                if KSUB <= 1:
                    return
                for hc in range(HC):
                    kg = wload(wg_d[hc])
                    ku = wload(wu_d[hc])
                    b = cnt["g"] % 2
                    cnt["g"] += 1
                    bg, bu = b, 2 + b
                    S.group("pe", [MM(banks[bg][:, :T], F.wring_t[:, kg, kc * 128:(kc + 1) * 128], ht[:, kc, :T], kc == 0, kc == DC - 1)
                                   for kc in range(DC)], reads=[F.wring.res[kg]] + F.R_ht, writes=[R_bank[bg]])
                    S.group("pe", [MM(banks[bu][:, :T], F.wring_t[:, ku, kc * 128:(kc + 1) * 128], ht[:, kc, :T], kc == 0, kc == DC - 1)
                                   for kc in range(DC)], reads=[F.wring.res[ku]] + F.R_ht, writes=[R_bank[bu]])
                    S.op("act", ACT(F.sgt[:, b, :T], banks[bg][:, :T], AF.Silu), reads=[R_bank[bg]], writes=[F.R_sg[b]])
                    S.op("dve", TTo(actt[:, hc, :T], F.sgt[:, b, :T], banks[bu][:, :T], ALU.mult),
                         reads=[F.R_sg[b], R_bank[bu]], writes=[F.R_act[hc]])
                if KSUB <= 2:
                    return
                for dc in range(DC):
                    k = F.wdring.next()
                    S.dma("pool", DMA(F.wdring_t[:, k, :], wd_d[dc]), f"{pf}wd{k}", writes=[F.wdring.res[k]])
                    by = 4 + cnt["y"] % 2
                    cnt["y"] += 1
                    S.group("pe", [MM(banks[by][:, :T], F.wdring_t[:, k, hc * 128:(hc + 1) * 128], actt[:, hc, :T], hc == 0, hc == HC - 1)
                                   for hc in range(HC)], reads=[F.wdring.res[k]] + F.R_act, writes=[R_bank[by]])
                    residual_stats(T, dc, by, mod(i0 + 2, dc, n))
                if KSUB <= 3:
                    return
                ln_finish(T, v, post)
            F.ffn = ffn
            return F

        with contextlib.suppress(_Skip), ExitStack() as es2:
            if KSTOP < 1:
                raise _Skip()
            F = make_ffn(es2, "a_")
            xt, ht = F.xt, F.ht

            def sb2(name, shape, dt):
                return es2.enter_context(nc.sbuf_tensor("t2_" + name, list(shape), dt))
            ropet = sb2("ropet", [128, 4, TT_], F32)
            qraw = sb2("qraw", [128, 2, TT_], F32R)
            rt1 = sb2("rt1", [128, 2, TT_], F32)
            rt2 = sb2("rt2", [128, 2, TT_], F32)
            outb = sb2("outb", [128, 4, TT_], BF)
            voutb = sb2("voutb", [128, 2, 4, 128], BF)
            R_rope = Res()
            R_qraw = [Res(), Res()]
            R_rt1 = [Res(), Res()]
            R_rt2 = [Res(), Res()]
            R_outb = [Res() for _ in range(4)]
            R_vout = [Res(), Res()]
            ca = {"p": 0, "r": 0, "o": 0, "v": 0}

            def proj_fm(T, oc):
                k = F.wload(win_d[oc])
                bp = ca["p"] % 4
                ca["p"] += 1
                S.group("pe", [MM(banks[bp][:, :T], F.wring_t[:, k, kc * 128:(kc + 1) * 128], ht[:, kc, :T], kc == 0, kc == DC - 1)
                               for kc in range(DC)], reads=[F.wring.res[k]] + F.R_ht, writes=[R_bank[bp]])
                return bp

            def store_fm(T, bo, dst_ap, dst_res):
                S.dma("sp", DMA(dst_ap, outb[:, bo, :T]), f"so{bo}", reads=[R_outb[bo]], writes=dst_res)

            def proj_rope(T, oc, which, dst_ap, dst_res):
                bp = proj_fm(T, oc)
                b = ca["r"] % 2
                ca["r"] += 1
                br = 4 + b
                S.op("act", ACT(qraw[:, b, :T], banks[bp][:, :T], AF.Copy), reads=[R_bank[bp]], writes=[R_qraw[b]])
                S.op("pe", MM(banks[br][:, :T], pmat[:, which, :], qraw[:, b, :T], True, True),
                     reads=[R_qraw[b], R_const], writes=[R_bank[br]])
                S.op("dve", TTo(rt1[:, b, :T], banks[bp][:, :T], ropet[:, 2 * which, :T], ALU.mult),
                     reads=[R_bank[bp], R_rope], writes=[R_rt1[b]])
                S.op("dve", TTo(rt2[:, b, :T], banks[br][:, :T], ropet[:, 2 * which + 1, :T], ALU.mult),
                     reads=[R_bank[br], R_rope], writes=[R_rt2[b]])
                bo = ca["o"] % 4
                ca["o"] += 1
                S.op("dve", TTo(outb[:, bo, :T], rt1[:, b, :T], rt2[:, b, :T], ALU.add),
                     reads=[R_rt1[b], R_rt2[b]], writes=[R_outb[bo]])
                store_fm(T, bo, dst_ap, dst_res)

            def proj_plain(T, oc, dst_ap, dst_res):
                bp = proj_fm(T, oc)
                bo = ca["o"] % 4
                ca["o"] += 1
                S.op("act", ACT(outb[:, bo, :T], banks[bp][:, :T], AF.Copy), reads=[R_bank[bp]], writes=[R_outb[bo]])
                store_fm(T, bo, dst_ap, dst_res)

            def proj_v(T, oc, dst_ap, dst_res):
                k = F.wload(win_d[oc])
                bv = 6 + ca["v"] % 2
                ntb = T // 128
                for tb in range(ntb):
                    S.group("pe", [MM(banks[bv][:, tb * 128:(tb + 1) * 128], ht[:, kc, tb * 128:(tb + 1) * 128],
                                      F.wring_t[:, k, kc * 128:(kc + 1) * 128], kc == 0, kc == DC - 1)
                                   for kc in range(DC)], reads=[F.wring.res[k]] + F.R_ht, writes=[R_bank[bv]])
                b = ca["v"] % 2
                ca["v"] += 1
                S.op("act", ACT(voutb[:, b, 0:ntb, :], banks[bv][:, 0:T].rearrange("p (a b) -> p a b", b=128), AF.Copy),
                     reads=[R_bank[bv]], writes=[R_vout[b]])
                S.dma("sp", DMA(dst_ap.rearrange("(a p) d -> p a d", p=128), voutb[:, b, 0:ntb, :]), f"sv{b}",
                      reads=[R_vout[b]], writes=dst_res)

            def vview(ap2d):
                return ap2d.rearrange("r (x d) -> (r x) d", d=128)

            for tile in KTILES:
                is_ctx = tile == 4
                T = CTX if is_ctx else TT_
                n = 1 if is_ctx else 0
                src = ctxT_d if is_ctx else xT_d
                t0 = 0 if is_ctx else tile * TT_
                S.dma("sp", DMA(xt[:, :, :T], src.rearrange("(c p) t -> p c t", p=128)[:, :, t0:t0 + T]), "ldx", writes=F.R_xt)
                if not is_ctx:
                    S.dma("sp", DMA(ropet[:, 0:2, :], ropeA_d.rearrange("a p t -> p a t")[:, :, t0:t0 + T]), "ldr", writes=[R_rope])
                    S.dma("sp", DMA(ropet[:, 2:4, :], ropeB_d.rearrange("a p t -> p a t")[:, :, t0:t0 + T]), "ldr", writes=[R_rope])
                F.ffn(T, n, 0, wg1_d, wu1_d, wd1_d, 0)
                if KSUB <= 4:
                    continue
                if not is_ctx:
                    S.dma("sp", DMA(x1_d.rearrange("(c p) t -> p c t", p=128)[:, :, t0:t0 + T], xt[:, :, :T]), "stx",
                          reads=F.R_xt, writes=[R_x1d[tile]])
                    if DEBUG_OUT:
                        S.dma("sp", DMA(dbg["x1"].rearrange("(c p) t -> p c t", p=128)[:, :, t0:t0 + T], xt[:, :, :T]), "stdbg",
                              reads=F.R_xt)
                F.modulate_to_h(T, 4, 3, n)
                if KSUB <= 5:
                    continue
                if is_ctx:
                    for j, oc in enumerate([8, 9] + list(range(20, 28))):
                        proj_plain(T, oc, kc_d[j], [R_kcd[j]])
                    for j, oc in enumerate([10, 11] + list(range(28, 36))):
                        proj_v(T, oc, vc_d[j], [R_vcd[j]])
                else:
                    for hq in range(8 if KPROJ & 1 else 0):
                        proj_rope(T, hq, 0, q_d[hq][:, t0:t0 + T], [R_qd[hq][tile]])
                    for g in range(2 if KPROJ & 1 else 0):
                        proj_rope(T, 8 + g, 0, L(tile, R_KA + g * 128), [R_kvl[tile]])
                    for h in range(8 if KPROJ & 1 else 0):
                        proj_rope(T, 12 + h, 1, q_d[8 + h][:, t0:t0 + T], [R_qd[8 + h][tile]])
                    for h in range(8 if KPROJ & 1 else 0):
                        proj_rope(T, 20 + h, 1, L(tile, R_KB + h * 128), [R_kvl[tile]])
                    for g in range(2 if KPROJ & 2 else 0):
                        proj_v(T, 10 + g, vview(L(tile, R_VA + g * 128)), [R_kvl[tile]])
                    for h in range(8 if KPROJ & 2 else 0):
                        proj_v(T, 28 + h, vview(L(tile, R_VB + h * 128)), [R_kvl[tile]])
                    d = {k_: v_ for k_, v_ in S.dcnt.items() if k_.startswith("so") or k_.startswith("sv")}
                    S._waits("pool", d)
                    if not NOCC and KSTOP >= 2:
                        for c3 in range(3):
                            S.dma("pool", (lambda tl, cc_: (lambda e: e.collective_compute(
                                "AllGather", ALU.bypass, replica_groups=[[0, 1, 2, 3], [4, 5, 6, 7]],
                                ins=[kvl_h[tl][cc_].ap().opt()], outs=[kva_h[tl][cc_].ap().opt()])))(tile, c3),
                                f"cc{tile}_{c3}", reads=[R_kvl[tile]], writes=[R_kva[tile]], inc=1)
        S.barrier()

        ot_d = nc.dram_tensor("ots", [16, 128, NLOC], BF).ap()
        R_otd = [[Res() for _ in range(NT)] for _ in range(16)]
        with contextlib.suppress(_Skip), ExitStack() as es3:
            if KSTOP < 3:
                raise _Skip()
            def sb3(name, shape, dt):
                return es3.enter_context(nc.sbuf_tensor("t3_" + name, list(shape), dt))
            NK = 256 + SEQ
            KT = sb3("KT", [128, 2, NK], BF)
            VT = sb3("VT", [128, 2, NKB_B, 128], BF)
            QT = sb3("QT", [128, 2, TT_], BF)
            pT = sb3("pT", [128, 8, TT_], BF)
            af1 = sb3("af1", [128, 2, TT_], F32)
            af2 = sb3("af2", [128, 2, TT_], F32)
            af3 = sb3("af3", [128, TT_], F32)
            osq = sb3("osq", [128, TT_], F32R)
            ob = sb3("ob", [128, 2, TT_], BF)
            lacc = sb3("lacc", [128, 2, TT_], F32)
            R_lacc = Res()
            R_KT = [Res(), Res()]
            R_VT = [Res(), Res()]
            R_QT = [Res(), Res()]
            R_pT = [Res() for _ in range(8)]
            R_af1 = [Res(), Res()]
            R_af2 = [Res(), Res()]
            R_af3 = Res()
            R_osq = Res()
            R_ob = [Res(), Res()]
            aa = {"kv": 0, "q": 0, "s": 0, "p": 0, "o": 0}
            SCALE_A = 128.0 ** -0.5
            SCALE_B = 64.0 ** -0.5

            def load_q(qc, gq):
                b = aa["q"] % 2
                aa["q"] += 1
                S.dma("sp", DMA(QT[:, b, :], q_d[qc][:, gq * TT_:(gq + 1) * TT_]), f"lq{b}",
                      reads=[R_qd[qc][gq]], writes=[R_QT[b]])
                return b

            def store_o(qc, gq, bo):
                S.dma("sp", DMA(ot_d[qc][:, gq * TT_:(gq + 1) * TT_], ob[:, bo, :]), f"sob{bo}",
                      reads=[R_ob[bo]], writes=[R_otd[qc][gq]])
                if DEBUG_OUT:
                    S.dma("sp", DMA(dbg["ot"][qc][:, gq * TT_:(gq + 1) * TT_], ob[:, bo, :]), f"sob{bo}", reads=[R_ob[bo]])

            for g in range(2):
                kb_ = aa["kv"] % 2
                aa["kv"] += 1
                S.dma("sp", DMA(KT[:, kb_, 0:CTX], kc_d[g]), f"lk{kb_}", reads=[R_kcd[g]], writes=[R_KT[kb_]])
                S.dma("sp", DMA(VT[:, kb_, 0:2, :], vc_d[g].rearrange("(a p) d -> p a d", p=128)), f"lv{kb_}",
                      reads=[R_vcd[g]], writes=[R_VT[kb_]])
                for t in range(NT):
                    S.dma("sp", DMA(KT[:, kb_, CTX + t * TT_:CTX + (t + 1) * TT_], L(t, R_KA + g * 128)),
                          f"lk{kb_}", reads=[R_kvl[t]], writes=[R_KT[kb_]])
                    S.dma("sp", DMA(VT[:, kb_, 2 + 4 * t:2 + 4 * (t + 1), :],
                                    vview(L(t, R_VA + g * 128)).rearrange("(a p) d -> p a d", p=128)),
                          f"lv{kb_}", reads=[R_kvl[t]], writes=[R_VT[kb_]])
                for r in range(4):
                    base = r * KVROWS
                    S.dma("sp", DMA(KT[:, kb_, CTX + NLOC + r * 128:CTX + NLOC + (r + 1) * 128],
                                    G(3, r, R_KA + g * 128)[:, 384:512]),
                          f"lk{kb_}", reads=[R_kva[3]], writes=[R_KT[kb_]])
                    S.dma("sp", DMA(KT[:, kb_, CTX + NLOC + (4 + r) * 128:CTX + NLOC + (5 + r) * 128],
                                    G(0, r, R_KA + g * 128)[:, 0:128]),
                          f"lk{kb_}", reads=[R_kva[0]], writes=[R_KT[kb_]])
                    S.dma("sp", DMA(VT[:, kb_, 18 + r, :],
                                    vview(G(3, r, R_VA + g * 128))[384:512, :]),
                          f"lv{kb_}", reads=[R_kva[3]], writes=[R_VT[kb_]])
                    S.dma("sp", DMA(VT[:, kb_, 22 + r, :],
                                    vview(G(0, r, R_VA + g * 128))[0:128, :]),
                          f"lv{kb_}", reads=[R_kva[0]], writes=[R_VT[kb_]])
                for gi in range(4):
                    qc = g * 4 + gi
                    for gq in range(4):
                        qb_ = load_q(qc, gq)
                        n0 = gq * 4
                        items = [(0, 0, TT_, None), (1, 0, TT_, None)]
                        for kbl in range(max(n0 - 1, 0), min(n0 + 4, 15) + 1):
                            qlo = max(kbl - 1, n0)
                            qhi = min(kbl + 1, n0 + 3)
                            mlo = (qlo - (kbl - 1)) * 128
                            mhi = (qhi - (kbl - 1) + 1) * 128
                            items.append((2 + kbl, (qlo - n0) * 128, (qhi - n0 + 1) * 128, mask3[:, mlo:mhi]))
                        if n0 == 0:
                            for r in range(4):
                                items.append((18 + r, 0, 128, medge[:, r, :]))
                        if n0 + 3 == 15:
                            for r in range(4):
                                items.append((22 + r, 384, 512, medge[:, 4 + r, :]))
                        nit = len(items)

                        def a_qk(ii):
                            kblk, c0, c1, msk = items[ii]
                            bs = aa["s"] % 4
                            aa["s"] += 1
                            S.op("pe", MM(banks[bs][:, c0:c1], KT[:, kb_, kblk * 128:(kblk + 1) * 128], QT[:, qb_, c0:c1], True, True),
                                 reads=[R_KT[kb_], R_QT[qb_]], writes=[R_bank[bs]])
                            bp = aa["p"] % 4
                            aa["p"] += 1
                            S.op("act", ACT(pT[:, bp, c0:c1], banks[bs][:, c0:c1], AF.Exp, scale=SCALE_A),
                                 reads=[R_bank[bs]], writes=[R_pT[bp]])
                            if msk is not None:
                                S.op("dve", TTo(pT[:, bp, c0:c1], pT[:, bp, c0:c1], msk, ALU.mult),
                                     reads=[R_pT[bp], R_const], writes=[R_pT[bp]])
                            return bp

                        def a_pv(ii, bp):
                            kblk, c0, c1, msk = items[ii]
                            S.op("pe", MM(banks[4][:, c0:c1], VT[:, kb_, kblk, :], pT[:, bp, c0:c1], ii == 0, ii == nit - 1),
                                 reads=[R_VT[kb_], R_pT[bp]], writes=[R_bank[4]])
                            S.op("pe", MM(banks[6][:, c0:c1], ones_b[:], pT[:, bp, c0:c1], ii == 0, ii == nit - 1),
                                 reads=[R_pT[bp], R_const], writes=[R_bank[6]])

                        pend = [a_qk(0), a_qk(1)]
                        for ii in range(nit):
                            if ii + 2 < nit:
                                pend.append(a_qk(ii + 2))
                            a_pv(ii, pend[ii])
                        b = aa["o"] % 2
                        aa["o"] += 1
                        S.op("dve", TS(af1[:, b, :], banks[6][:, :], esink[:, qc:qc + 1], None, ALU.add),
                             reads=[R_bank[6], R_const], writes=[R_af1[b]])
                        S.op("dve", RCP(af1[:, b, :], af1[:, b, :]), reads=[R_af1[b]], writes=[R_af1[b]])
                        S.op("dve", TTo(ob[:, b, :], banks[4][:, :], af1[:, b, :], ALU.mult),
                             reads=[R_bank[4], R_af1[b]], writes=[R_ob[b]])
                        store_o(qc, gq, b)

            deferred = []
            for h in range(8):
                kb_ = aa["kv"] % 2
                aa["kv"] += 1
                S.dma("sp", DMA(KT[:, kb_, 0:CTX], kc_d[2 + h]), f"lk{kb_}", reads=[R_kcd[2 + h]], writes=[R_KT[kb_]])
                S.dma("sp", DMA(VT[:, kb_, 0:2, :], vc_d[2 + h].rearrange("(a p) d -> p a d", p=128)), f"lv{kb_}",
                      reads=[R_vcd[2 + h]], writes=[R_VT[kb_]])
                for r in range(4):
                    base = r * KVROWS
                    for t in range(NT):
                        k0 = CTX + r * NLOC + t * TT_
                        S.dma("sp", DMA(KT[:, kb_, k0:k0 + TT_], G(t, r, R_KB + h * 128)),
                              f"lk{kb_}", reads=[R_kva[t]], writes=[R_KT[kb_]])
                        S.dma("sp", DMA(VT[:, kb_, k0 // 128:k0 // 128 + 4, :],
                                        vview(G(t, r, R_VB + h * 128)).rearrange("(a p) d -> p a d", p=128)),
                              f"lv{kb_}", reads=[R_kva[t]], writes=[R_VT[kb_]])
                qc = 8 + h
                for gq in range(4):
                    qb_ = load_q(qc, gq)
                    def b_qk(kblk):
                        bs0 = (aa["s"] % 2) * 2
                        aa["s"] += 1
                        bs1 = bs0 + 1
                        S.op("pe", MM(banks[bs0][:, :], KT[0:64, kb_, kblk * 128:(kblk + 1) * 128], QT[0:64, qb_, :], True, True),
                             reads=[R_KT[kb_], R_QT[qb_]], writes=[R_bank[bs0]])
                        S.op("pe", MM(banks[bs1][:, :], KT[64:128, kb_, kblk * 128:(kblk + 1) * 128], QT[64:128, qb_, :], True, True),
                             reads=[R_KT[kb_], R_QT[qb_]], writes=[R_bank[bs1]])
                        bp0 = (aa["p"] % 4) * 2
                        aa["p"] += 1
                        bp1 = bp0 + 1
                        S.op("act", ACT(pT[:, bp0, :], banks[bs0][:, :], AF.Exp, scale=SCALE_B), reads=[R_bank[bs0]], writes=[R_pT[bp0]])
                        S.op("act", ACT(pT[:, bp1, :], banks[bs1][:, :], AF.Exp, scale=SCALE_B), reads=[R_bank[bs1]], writes=[R_pT[bp1]])
                        return bp0, bp1

                    def b_pv(kblk, bp0, bp1):
                        first, last = kblk == 0, kblk == NKB_B - 1
                        S.op("pe", MM(banks[4][:, :], VT[:, kb_, kblk, :], pT[:, bp0, :], first, last),
                             reads=[R_VT[kb_], R_pT[bp0]], writes=[R_bank[4]])
                        S.op("pe", MM(banks[5][:, :], VT[:, kb_, kblk, :], pT[:, bp1, :], first, last),
                             reads=[R_VT[kb_], R_pT[bp1]], writes=[R_bank[5]])
                        S.op("pe", MM(banks[7][:, :], ones_b[:], pT[:, bp1, :], first, last),
                             reads=[R_pT[bp1], R_const], writes=[R_bank[7]])
                        if first:
                            S.op("dve", CP(lacc[:, 0, :], pT[:, bp0, :]), reads=[R_pT[bp0]], writes=[R_lacc])
                        else:
                            S.op("dve", TTo(lacc[:, 0, :], lacc[:, 0, :], pT[:, bp0, :], ALU.add),
                                 reads=[R_pT[bp0], R_lacc], writes=[R_lacc])
                        if last:
                            S.op("pe", MM(banks[6][:, :], onesf[:], lacc[:, 0, :], True, True), reads=[R_lacc, R_onesf], writes=[R_bank[6]])

                    cur = b_qk(0)
                    for kblk in range(NKB_B):
                        nxt = b_qk(kblk + 1) if kblk + 1 < NKB_B else None
                        b_pv(kblk, *cur)
                        cur = nxt
                        if kblk == 3 and deferred:
                            deferred.pop(0)()
                    S.op("dve", RCP(af1[:, 0, :], banks[6][:, :]), reads=[R_bank[6]], writes=[R_af1[0]])
                    S.op("dve", RCP(af1[:, 1, :], banks[7][:, :]), reads=[R_bank[7]], writes=[R_af1[1]])
                    S.op("dve", TTo(af2[:, 0, :], banks[4][:, :], af1[:, 0, :], ALU.mult), reads=[R_bank[4], R_af1[0]], writes=[R_af2[0]])
                    S.op("dve", TTo(af2[:, 1, :], banks[5][:, :], af1[:, 1, :], ALU.mult), reads=[R_bank[5], R_af1[1]], writes=[R_af2[1]])
                    S.op("dve", STT(af3[:, :], af2[:, 1, :], lamt[:, 1:2], af2[:, 0, :], ALU.mult, ALU.add),
                         reads=[R_af2[0], R_af2[1], R_const], writes=[R_af3])

                    def part2(qc=qc, gq=gq):
                        S.op("act", ACT(osq[:, :], af3[:, :], AF.Square), reads=[R_af3], writes=[R_osq])
                        S.op("pe", MM(banks[6][:, :], ones_r[:], osq[:, :], True, True), reads=[R_osq, R_const], writes=[R_bank[6]])
                        S.op("act", ACT(af1[:, 0, :], banks[6][:, :], AF.Sqrt, bias=epsln[:, 1:2], scale=1.0 / 128.0),
                             reads=[R_bank[6], R_const], writes=[R_af1[0]])
                        S.op("dve", RCP(af1[:, 0, :], af1[:, 0, :]), reads=[R_af1[0]], writes=[R_af1[0]])
                        S.op("dve", TTo(af2[:, 0, :], af3[:, :], af1[:, 0, :], ALU.mult), reads=[R_af3, R_af1[0]], writes=[R_af2[0]])
                        b = aa["o"] % 2
                        aa["o"] += 1
                        S.op("dve", TS(ob[:, b, :], af2[:, 0, :], subg[:, 0:1], 1.0 - LAM_INIT, ALU.mult, ALU.mult),
                             reads=[R_af2[0], R_const], writes=[R_ob[b]])
                        store_o(qc, gq, b)
                    deferred.append(part2)
            while deferred:
                deferred.pop(0)()
        S.barrier()

        out_toks = []
        with contextlib.suppress(_Skip), ExitStack() as es4:
            if KSTOP < 4:
                raise _Skip()
            F = make_ffn(es4, "c_")
            xt, ht, actt = F.xt, F.ht, F.actt

            def sb4(name, shape, dt):
                return es4.enter_context(nc.sbuf_tensor("t4_" + name, list(shape), dt))
            sga = sb4("sga", [128, TT_], F32)
            sgb = sb4("sgb", [128, TT_], F32)
            mt1 = sb4("mt1", [128, TT_], F32)
            mt2 = sb4("mt2", [128, TT_], F32)
            R_sga, R_sgb, R_mt1, R_mt2 = Res(), Res(), Res(), Res()
            T = TT_
            for tile in range(NT):
                t0 = tile * TT_
                S.dma("sp", DMA(xt[:, :, :], x1_d.rearrange("(c p) t -> p c t", p=128)[:, :, t0:t0 + T]), "ldx2",
                      reads=[R_x1d[tile]], writes=F.R_xt)
                S.dma("sp", DMA(actt[:, 16:32, :], ot_d.rearrange("c p t -> p c t")[:, :, t0:t0 + T]), "ldo",
                      reads=[R_otd[c][tile] for c in range(16)], writes=F.R_act[16:32])
                F.modulate_to_h(T, 4, 3, 0)
                F.scale_x(T)
                for j in range(DC):
                    kga = F.wload(win_d[36 + j])
                    kgb = F.wload(win_d[52 + j])
                    kba = F.wload(wba_d[j], 1024)
                    kbb = F.wload(wbb_d[j], 1024)
                    S.group("pe", [MM(banks[0][:, :], F.wring_t[:, kga, kc * 128:(kc + 1) * 128], ht[:, kc, :], kc == 0, kc == DC - 1)
                                   for kc in range(DC)], reads=[F.wring.res[kga]] + F.R_ht, writes=[R_bank[0]])
                    S.group("pe", [MM(banks[1][:, :], F.wring_t[:, kgb, kc * 128:(kc + 1) * 128], ht[:, kc, :], kc == 0, kc == DC - 1)
                                   for kc in range(DC)], reads=[F.wring.res[kgb]] + F.R_ht, writes=[R_bank[1]])
                    S.group("pe", [MM(banks[2][:, :], F.wring_t[:, kba, kc * 128:(kc + 1) * 128], actt[:, 16 + kc, :], kc == 0, kc == 7)
                                   for kc in range(8)], reads=[F.wring.res[kba]] + F.R_act[16:24], writes=[R_bank[2]])
                    S.group("pe", [MM(banks[3][:, :], F.wring_t[:, kbb, kc * 128:(kc + 1) * 128], actt[:, 24 + kc, :], kc == 0, kc == 7)
                                   for kc in range(8)], reads=[F.wring.res[kbb]] + F.R_act[24:32], writes=[R_bank[3]])
                    S.op("act", ACT(sga[:, :], banks[0][:, :], AF.Sigmoid), reads=[R_bank[0]], writes=[R_sga])
                    S.op("act", ACT(sgb[:, :], banks[1][:, :], AF.Sigmoid), reads=[R_bank[1]], writes=[R_sgb])
                    S.op("dve", TTo(mt1[:, :], sga[:, :], banks[2][:, :], ALU.mult), reads=[R_sga, R_bank[2]], writes=[R_mt1])
                    S.op("dve", TTo(mt2[:, :], sgb[:, :], banks[3][:, :], ALU.mult), reads=[R_sgb, R_bank[3]], writes=[R_mt2])
                    S.op("dve", TTo(actt[:, j, :], mt1[:, :], mt2[:, :], ALU.add), reads=[R_mt1, R_mt2], writes=[F.R_act[j]])
                for dc in range(DC):
                    k = F.wload(wout_d[dc])
                    by = 4 + dc % 2
                    S.group("pe", [MM(banks[by][:, :], F.wring_t[:, k, kc * 128:(kc + 1) * 128], actt[:, kc, :], kc == 0, kc == DC - 1)
                                   for kc in range(DC)], reads=[F.wring.res[k]] + F.R_act[0:16], writes=[R_bank[by]])
                    F.residual_stats(T, dc, by, mod(5, dc, 0))
                F.ln_finish(T, 2)
                F.ffn(T, 0, 6, wg2_d, wu2_d, wd2_d, 4)
                out_toks.append(S.dma("sp", DMA(outT_d.rearrange("(c p) t -> p c t", p=128)[:, :, t0:t0 + T], xt[:, :, :]), "sto",
                                      reads=F.R_xt))
        if DEBUG_OUT and KSTOP >= 1:
            S.barrier()
            S.dma("sp", DMA(dbg["q"][:, :, 0:TT_], q_d[:, :, 0:TT_]), "stdbg")
        if KSTOP < 4:
            S.barrier()
            srcd = (x1_d if KSUB >= 5 else xT_d) if KSTOP >= 1 else wmod_d[0:D, 0:NLOC]
            if len(KTILES) < 5 and KSTOP >= 1:
                S.dma("sp", DMA(outT_d[:, 0:TT_], srcd[:, 0:TT_]), "sto")
            else:
                S.dma("sp", DMA(outT_d[:, :], srcd[:, :] if KSTOP >= 1 else srcd), "sto")
        S.barrier()

        sem_names = list(ENGS) + sorted(S.dcnt.keys())
        sems = {nm: es.enter_context(nc.semaphore("s_" + nm)) for nm in sem_names}
        block = es.enter_context(nc.Block())

        def replay(eng_name):
            def body(e):
                for it in S.prog[eng_name]:
                    if it[0] == "w":
                        e.wait_ge(sems[it[1]], it[2])
                    else:
                        ins = it[1](e)
                        if it[2] is not None:
                            if it[2].startswith("cc"):
                                ins.then_inc(sems[it[2]])
                            else:
                                ins.then_inc(sems[it[2]], it[3])
            return body
        block.tensor(replay("pe"))
        block.scalar(replay("act"))
        block.vector(replay("dve"))
        block.gpsimd(replay("pool"))
        block.sync(replay("sp"))
    return nc


def _tile_w(W):
    K, N = W.shape
    return np.ascontiguousarray(W.reshape(K // 128, 128, N // 128, 128).transpose(2, 1, 0, 3).reshape(N // 128, 128, K))


def _cols(v):
    return np.ascontiguousarray(np.asarray(v, np.float32).reshape(-1, 128).T)


def _rope_tables(pos, dim):
    quarter = dim // 4
    inv = (10000.0 ** (-np.arange(quarter, dtype=np.float32) / quarter)).astype(np.float32)
    row = (pos // 64).astype(np.float32)
    col = (pos % 64).astype(np.float32)
    ang = np.concatenate([row[:, None] * inv, col[:, None] * inv], axis=-1).astype(np.float32)
    cos, sin = np.cos(ang).astype(np.float32), np.sin(ang).astype(np.float32)
    half = dim // 2
    idx = np.arange(128) % half
    return np.ascontiguousarray(np.stack([cos[:, idx].T, sin[:, idx].T]).astype(np.float32))


def _pmat(dim):
    half = dim // 2
    M = np.zeros((128, 128), np.float32)
    for p in range(128):
        i = p % dim
        if i < half:
            M[p, p + half] = -1.0
        else:
            M[p, p - half] = 1.0
    return np.ascontiguousarray(M.T)


_CACHE = {}


def kernel(x, c, ctx, c_ctx, w_mod, b_mod, ffn1_w_gate, ffn1_w_up, ffn1_w_down, ln1_g, ln1_b,
           w_in, a_sink, b_lambda_q1, b_lambda_k1, b_lambda_q2, b_lambda_k2, b_subln_g,
           w_branch_a, w_branch_b, w_out, ln2_g, ln2_b, ffn2_w_gate, ffn2_w_up, ffn2_w_down,
           ln3_g, ln3_b):
    f = lambda a: np.asarray(a, np.float32)
    x, c, ctx, c_ctx = f(x), f(c), f(ctx), f(c_ctx)
    shared = {
        "bmodc": _cols(f(b_mod)[0]),
        "lnp": np.ascontiguousarray(np.concatenate([_cols(f(v)[0]) for v in (ln1_g, ln1_b, ln2_g, ln2_b, ln3_g, ln3_b)], axis=1)),
        "subg": np.ascontiguousarray(f(b_subln_g)[0].reshape(128, 1)),
        "sinkb": np.ascontiguousarray(np.broadcast_to(f(a_sink)[0][None, :], (128, 8))),
        "lamv": np.ascontiguousarray(np.broadcast_to(np.concatenate(
            [f(b_lambda_q1)[0], f(b_lambda_k1)[0], f(b_lambda_q2)[0], f(b_lambda_k2)[0]])[None, :], (128, 256))),
        "w_mod": np.ascontiguousarray(f(w_mod)[0]),
        "wg1": _tile_w(f(ffn1_w_gate)[0]), "wu1": _tile_w(f(ffn1_w_up)[0]), "wd1": _tile_w(f(ffn1_w_down)[0]),
        "wg2": _tile_w(f(ffn2_w_gate)[0]), "wu2": _tile_w(f(ffn2_w_up)[0]), "wd2": _tile_w(f(ffn2_w_down)[0]),
        "w_in": _tile_w(f(w_in)[0]), "w_ba": _tile_w(f(w_branch_a)[0]), "w_bb": _tile_w(f(w_branch_b)[0]),
        "w_out": _tile_w(f(w_out)[0]),
        "pmat": np.ascontiguousarray(np.stack([_pmat(128), _pmat(64)])),
        "ident": np.eye(2, dtype=np.float32),
    }
    jj, ii = np.meshgrid(np.arange(128), np.arange(128), indexing="ij")
    tri_le = (jj <= ii).astype(np.float32)
    tri_ge = (jj >= ii).astype(np.float32)
    shared["mask3"] = np.ascontiguousarray(np.concatenate([tri_le, np.ones((128, 128), np.float32), tri_ge], axis=1))
    in_maps = []
    for r in range(8):
        b, qd = r // 4, r % 4
        m = dict(shared)
        m["xT"] = np.ascontiguousarray(x[b, qd * NLOC:(qd + 1) * NLOC, :].T)
        m["ctxT"] = np.ascontiguousarray(ctx[b].T)
        m["cc"] = np.ascontiguousarray(np.stack([_cols(c[b]), _cols(c_ctx)], axis=2).reshape(128, 32))
        pos = np.arange(qd * NLOC, (qd + 1) * NLOC)
        m["ropeA"] = _rope_tables(pos, 128)
        m["ropeB"] = _rope_tables(pos, 64)
        me = np.zeros((128, 8, 128), np.float32)
        if qd > 0:
            me[:, qd - 1, :] = tri_ge
        if qd < 3:
            me[:, 4 + qd + 1, :] = tri_le
        m["medge"] = np.ascontiguousarray(me.reshape(128, 1024))
        in_maps.append(m)
    if os.environ.get("KSIM"):
        _CACHE["in_maps"] = in_maps
        return None
    if "nc" not in _CACHE:
        _CACHE["nc"] = build_program()
    ncores = int(os.environ.get("KCORES", "8"))
    res = run_bass_kernel_spmd(_CACHE["nc"], in_maps[:ncores], core_ids=list(range(ncores)))
    _CACHE["last"] = res
    out = np.empty((2, SEQ, D), np.float32)
    for r in range(ncores):
        b, qd = r // 4, r % 4
        out[b, qd * NLOC:(qd + 1) * NLOC, :] = np.asarray(res.results[r]["outT"], np.float32).T
    return out
```
